# Optimizing a Trainium2 kernel written in Bass

```python
import math
import jax, jax.numpy as jnp
from jax import lax
import numpy as np

D_MODEL = 1024
BATCH = 16
SEQ = 256
DEPTH = 4
DEC_BATCH = 4
DEC_SEQ = 4096
PAST_LEN = 256

GRID_W = 64
N_MIXERS = 4
RMS_EPS = 1e-6
CONV_W = 3
POOL_WINDOWS = (2, 4, 8, 16)
POOL_GROUP = D_MODEL // len(POOL_WINDOWS)
SGU_CHUNK = 128
SGU_GROUPS = 8
SGU_DIM = D_MODEL // SGU_GROUPS
HGRN_HEADS = 8
HGRN_DK = D_MODEL // HGRN_HEADS
HGRN_DV = D_MODEL // HGRN_HEADS
HGRN_CHUNK = 32
D_FF = 2816
POS_BASE = 10000.0

kernel_name = 'hybrid_diffusion_interleaved_step'


def _layers_of(kind):
    return len(range(kind, DEPTH, N_MIXERS))


def rmsnorm(x, g):
    xf = x.astype(jnp.float32)
    y = xf * lax.rsqrt(jnp.mean(xf * xf, axis=-1, keepdims=True) + RMS_EPS)
    return (y * g.astype(jnp.float32)).astype(x.dtype)


def adaln_params(cvec, w, b):
    m = jax.nn.silu(cvec) @ w + b
    return jnp.split(m[:, None, :], 6, axis=-1)


def modulate(h, shift, scale):
    return h * (1.0 + scale) + shift


def dwconv3(h, w):
    hp = jnp.pad(h, ((0, 0), (1, 1), (0, 0)))
    return hp[:, :-2] * w[0] + hp[:, 1:-1] * w[1] + hp[:, 2:] * w[2]


def grid_pos_embed(n_tokens):
    rows = n_tokens // GRID_W
    rr, cc = jnp.meshgrid(jnp.arange(rows, dtype=jnp.float32), jnp.arange(GRID_W, dtype=jnp.float32), indexing='ij')
    rr, cc = rr.reshape(-1, 1), cc.reshape(-1, 1)
    quarter = D_MODEL // 4
    freq = jnp.exp(-math.log(POS_BASE) * jnp.arange(quarter, dtype=jnp.float32) / quarter)
    return jnp.concatenate([jnp.sin(rr * freq), jnp.cos(rr * freq), jnp.sin(cc * freq), jnp.cos(cc * freq)], axis=-1)


def short_conv_mixer(h, w_in, w_dw, w_out):
    bg, cg, xh = jnp.split(h @ w_in, 3, axis=-1)
    return (bg * dwconv3(cg * xh, w_dw)) @ w_out


def pool_mixer(h, w_group, scale):
    B, T, D = h.shape
    hf = h.astype(jnp.float32)
    csum = jnp.concatenate([jnp.zeros((B, 1, D), jnp.float32), jnp.cumsum(hf, axis=1)], axis=1)
    t = jnp.arange(T)
    outs = []
    for g, w in enumerate(POOL_WINDOWS):
        lo = jnp.maximum(t - w // 2, 0)
        hi = jnp.minimum(t + w // 2, T)
        sl = slice(g * POOL_GROUP, (g + 1) * POOL_GROUP)
        cg = csum[:, :, sl]
        mean = (cg[:, hi] - cg[:, lo]) / (hi - lo).astype(jnp.float32)[None, :, None]
        pooled = (mean - hf[:, :, sl]).astype(h.dtype)
        outs.append(pooled @ w_group[g])
    return jnp.concatenate(outs, axis=-1) * scale


def sgu_mixer(h, w_in, norm_g, w_s, b_s, w_out):
    B, T, D = h.shape
    u, v = jnp.split(jax.nn.gelu(h @ w_in), 2, axis=-1)
    v = rmsnorm(v, norm_g).reshape(B, T // SGU_CHUNK, SGU_CHUNK, SGU_GROUPS, SGU_DIM)
    s = jnp.einsum('gpq,bnqgc->bnpgc', w_s, v) + b_s.T[None, None, :, :, None]
    return (u * s.reshape(B, T, D)) @ w_out


def hgrn_lower_bounds(lb_param):
    p = jax.nn.softmax(lb_param.astype(jnp.float32), axis=0)
    return jnp.cumsum(p, axis=0) - p[0:1]


def hgrn_scan(q, k, v, logf, s0):
    B, T, H, _ = q.shape

    def to_chunks(a):
        return a.reshape(B, T // HGRN_CHUNK, HGRN_CHUNK, H, a.shape[-1]).transpose(1, 0, 3, 2, 4)

    mask = jnp.tril(jnp.ones((HGRN_CHUNK, HGRN_CHUNK), dtype=bool))

    def step(S, xs):
        qc, kc, vc, lfc = xs
        bc = jnp.cumsum(lfc, axis=2)
        o_inter = jnp.einsum('bhtd,bhde->bhte', qc * jnp.exp(bc), S)
        diff = bc[:, :, :, None, :] - bc[:, :, None, :, :]
        decay = jnp.where(mask[:, :, None], jnp.exp(jnp.minimum(diff, 0.0)), 0.0)
        scores = jnp.einsum('bhtd,bhtsd,bhsd->bhts', qc, decay, kc)
        o_intra = jnp.einsum('bhts,bhse->bhte', scores, vc)
        last = bc[:, :, -1:, :]
        S_new = jnp.exp(last[:, :, 0])[..., None] * S + jnp.einsum('bhsd,bhse->bhde', kc * jnp.exp(last - bc), vc)
        return S_new, o_inter + o_intra

    s_final, o = lax.scan(step, s0, (to_chunks(q), to_chunks(k), to_chunks(v), to_chunks(logf)))
    o = o.transpose(1, 0, 3, 2, 4).reshape(B, T, H, -1)
    return o, s_final


def hgrn_mixer(h, w_in, lb, norm_g, w_out, s0):
    B, T, D = h.shape
    q, zf, zb, inp, g = jnp.split(h @ w_in, 5, axis=-1)

    def heads(a):
        return a.reshape(B, T, HGRN_HEADS, -1).astype(jnp.float32)

    def gates(z, lb_d):
        lbh = lb_d.reshape(HGRN_HEADS, HGRN_DK)
        logf = jnp.logaddexp(jnp.log(lbh), jnp.log1p(-lbh) + jax.nn.log_sigmoid(z))
        k = (1.0 - lbh) * jax.nn.sigmoid(-z)
        return logf, k

    qh, ih = heads(q), heads(inp)
    lf_f, k_f = gates(heads(zf), lb[0])
    lf_b, k_b = gates(heads(zb), lb[1])
    s0 = s0.astype(jnp.float32)
    o_f, s_f = hgrn_scan(qh, k_f, ih, lf_f, s0[:, 0])
    o_b, s_b = hgrn_scan(qh[:, ::-1], k_b[:, ::-1], ih[:, ::-1], lf_b[:, ::-1], s0[:, 1])
    o = o_f + o_b[:, ::-1]
    o = rmsnorm(o, norm_g.reshape(HGRN_HEADS, HGRN_DV)).astype(h.dtype).reshape(B, T, D)
    return (o * jax.nn.silu(g)) @ w_out, jnp.stack([s_f, s_b], axis=1)


def conv_ffn(h, w_up, w_dw, w_down):
    a, b = jnp.split(dwconv3(h @ w_up, w_dw), 2, axis=-1)
    return (jax.nn.silu(a) * b) @ w_down


def setup_inputs(seed: int = 0) -> dict:
    key = jax.random.key(seed)
    ks = iter(jax.random.split(key, 32))

    def nrm(shape, scale):
        return scale * jax.random.normal(next(ks), shape, jnp.float32)

    n_a, n_b, n_c, n_d = (_layers_of(k) for k in range(N_MIXERS))
    D = D_MODEL
    return {
        'x_prompt': nrm((BATCH, SEQ, D), 1.0),
        'x_sample': nrm((DEC_BATCH, DEC_SEQ, D), 1.0),
        'state_rec': nrm((DEC_BATCH, n_d, 2, HGRN_HEADS, HGRN_DK, HGRN_DV), 0.5),
        'c': nrm((DEC_BATCH, D), 1.0),
        'c_ctx': nrm((D,), 1.0),
        'ada_w': nrm((DEPTH, D, 6 * D), 0.5 * D ** -0.5),
        'ada_b': nrm((DEPTH, 6 * D), 0.01),
        'norm_g': 1.0 + nrm((DEPTH, 2, D), 0.1),
        'final_g': 1.0 + nrm((D,), 0.1),
        'conv_w_in': nrm((n_a, D, 3 * D), D ** -0.5),
        'conv_w_dw': nrm((n_a, CONV_W, D), CONV_W ** -0.5),
        'conv_w_out': nrm((n_a, D, D), D ** -0.5),
        'pool_w': nrm((n_b, len(POOL_WINDOWS), POOL_GROUP, POOL_GROUP), POOL_GROUP ** -0.5),
        'pool_scale': 1.0 + nrm((n_b, D), 0.1),
        'sgu_w_in': nrm((n_c, D, 2 * D), D ** -0.5),
        'sgu_norm_g': 1.0 + nrm((n_c, D), 0.1),
        'sgu_w_s': nrm((n_c, SGU_GROUPS, SGU_CHUNK, SGU_CHUNK), SGU_CHUNK ** -0.5),
        'sgu_b_s': 1.0 + nrm((n_c, SGU_GROUPS, SGU_CHUNK), 0.1),
        'sgu_w_out': nrm((n_c, D, D), D ** -0.5),
        'hgrn_w_in': nrm((n_d, D, 5 * D), D ** -0.5),
        'hgrn_lb': nrm((DEPTH, 2, D), 0.1),
        'hgrn_norm_g': 1.0 + nrm((n_d, D), 0.1),
        'hgrn_w_out': nrm((n_d, D, D), D ** -0.5),
        'ffn_w_up': nrm((DEPTH, D, 2 * D_FF), D ** -0.5),
        'ffn_w_dw': nrm((DEPTH, CONV_W, 2 * D_FF), CONV_W ** -0.5),
        'ffn_w_down': nrm((DEPTH, D_FF, D), D_FF ** -0.5),
    }


def reference(x_prompt, x_sample, state_rec, c, c_ctx, ada_w, ada_b, norm_g, final_g,
              conv_w_in, conv_w_dw, conv_w_out, pool_w, pool_scale,
              sgu_w_in, sgu_norm_g, sgu_w_s, sgu_b_s, sgu_w_out,
              hgrn_w_in, hgrn_lb, hgrn_norm_g, hgrn_w_out,
              ffn_w_up, ffn_w_dw, ffn_w_down):
    xp = x_prompt
    xs = x_sample + grid_pos_embed(x_sample.shape[1]).astype(x_sample.dtype)[None]
    lbs = hgrn_lower_bounds(hgrn_lb)
    ctx_states = []
    for i in range(DEPTH):
        kind, j = i % N_MIXERS, i // N_MIXERS
        mp = adaln_params(c_ctx[None], ada_w[i], ada_b[i])
        ms = adaln_params(c, ada_w[i], ada_b[i])
        hp = modulate(rmsnorm(xp, norm_g[i, 0]), mp[0], mp[1])
        hs = modulate(rmsnorm(xs, norm_g[i, 0]), ms[0], ms[1])
        if kind == 0:
            yp = short_conv_mixer(hp, conv_w_in[j], conv_w_dw[j], conv_w_out[j])
            ys = short_conv_mixer(hs, conv_w_in[j], conv_w_dw[j], conv_w_out[j])
        elif kind == 1:
            yp = pool_mixer(hp, pool_w[j], pool_scale[j])
            ys = pool_mixer(hs, pool_w[j], pool_scale[j])
        elif kind == 2:
            yp = sgu_mixer(hp, sgu_w_in[j], sgu_norm_g[j], sgu_w_s[j], sgu_b_s[j], sgu_w_out[j])
            ys = sgu_mixer(hs, sgu_w_in[j], sgu_norm_g[j], sgu_w_s[j], sgu_b_s[j], sgu_w_out[j])
        else:
            zero_state = jnp.zeros((xp.shape[0], 2, HGRN_HEADS, HGRN_DK, HGRN_DV), jnp.float32)
            yp, sp = hgrn_mixer(hp, hgrn_w_in[j], lbs[i], hgrn_norm_g[j], hgrn_w_out[j], zero_state)
            ys, _ = hgrn_mixer(hs, hgrn_w_in[j], lbs[i], hgrn_norm_g[j], hgrn_w_out[j], state_rec[:, j])
            ctx_states.append(sp)
        xp = xp + mp[2] * yp
        xs = xs + ms[2] * ys
        hp = modulate(rmsnorm(xp, norm_g[i, 1]), mp[3], mp[4])
        hs = modulate(rmsnorm(xs, norm_g[i, 1]), ms[3], ms[4])
        xp = xp + mp[5] * conv_ffn(hp, ffn_w_up[i], ffn_w_dw[i], ffn_w_down[i])
        xs = xs + ms[5] * conv_ffn(hs, ffn_w_up[i], ffn_w_dw[i], ffn_w_down[i])
    new_state_rec = jnp.stack(ctx_states, axis=1)
    y_prompt = rmsnorm(xp, final_g)
    y_sample = rmsnorm(xs, final_g)
    return (y_prompt, y_sample, new_state_rec)
```

```python
import os
from contextlib import ExitStack
import numpy as np
import concourse.bass as bass
import concourse.mybir as mybir
from concourse.bass_utils import run_bass_kernel_spmd

F32, BF16 = mybir.dt.float32, mybir.dt.bfloat16
AF = mybir.ActivationFunctionType
ALU = mybir.AluOpType

D = 1024
H = 140
OWN = 2048
NS = OWN + 2 * H
OWN0, OWN1 = H, H + OWN
P0, P1 = NS, NS + 256
NT = NS + 512
DFF = 2816
NSLOT = 7
SLOT = 4096
EPS = 1e-6
ENGS = ["pe", "act", "dve", "pool", "sp"]
NL = int(os.environ.get("MK_LAYERS", "4"))
NCORES = int(os.environ.get("MK_CORES", "8"))
SKIP = os.environ.get("MK_SKIP", "")


class Op:
    __slots__ = ("eng", "fn", "deps", "sig", "dsem", "dval", "ssem", "sval", "dinc", "dmaw", "cost", "lat", "gidx", "epoch", "isfence", "preds", "npend", "ready", "succ", "done", "fin", "grp", "tag")


class Rec:
    def __init__(self):
        self.ops = {e: [] for e in ENGS}
        self.lastw = {}
        self.readers = {}
        self.dcnt = {}
        self.gcnt = 0
        self.epoch = 0
        self.tag = None

    def add(self, eng, fn, r=(), w=(), dma=None, inc=16, cost=None, lat=0.0):
        op = Op()
        op.cost = cost if cost is not None else (0.1 if eng in ('pool', 'sp') else 0.5)
        op.lat, op.gidx, op.epoch, op.isfence = lat, self.gcnt, self.epoch, False
        op.grp = None
        op.tag = self.tag
        self.gcnt += 1
        op.eng, op.fn, op.deps, op.sig, op.dsem, op.dval = eng, fn, set(), False, dma, 0
        op.dinc = inc
        op.dmaw = {}
        for k in r:
            x = self.lastw.get(k)
            if x is not None:
                op.deps.add(x)
        for k in w:
            x = self.lastw.get(k)
            if x is not None:
                op.deps.add(x)
            for rd in self.readers.get(k, ()):
                op.deps.add(rd)
        for k in r:
            self.readers.setdefault(k, []).append(op)
        for k in w:
            self.lastw[k] = op
            self.readers[k] = []
        op.deps.discard(op)
        for d_ in op.deps:
            if d_.dsem is not None:
                op.dmaw[d_.dsem] = self.dcnt[d_.dsem]
        if dma is not None:
            self.dcnt[dma] = self.dcnt.get(dma, 0) + inc
            op.dval = self.dcnt[dma]
        self.ops[eng].append(op)
        return op

    def fence(self):
        self.epoch += 1
        outs = []
        for e in ENGS:
            op = self.add(e, None, cost=0.0)
            op.isfence = True
            for k_, v_ in self.dcnt.items():
                op.dmaw[k_] = v_
            outs.append(op)
        return outs


def schedule(R, W=40, SYNC=0.35):
    import collections
    allops = sorted((op for e in ENGS for op in R.ops[e]), key=lambda o: o.gidx)
    dma_by_sem = collections.defaultdict(list)
    for op in allops:
        if op.dsem is not None:
            dma_by_sem[op.dsem].append(op)
    for op in allops:
        preds = set(op.deps)
        for sem, val in op.dmaw.items():
            for d in dma_by_sem[sem]:
                if d.dval <= val:
                    if d is not op:
                        preds.add(d)
                else:
                    break
        op.preds, op.succ, op.done, op.ready, op.fin = preds, [], False, 0.0, 0.0
    for op in allops:
        for p in op.preds:
            p.succ.append(op)
    new = {e: [] for e in ENGS}
    tnow = 0.0
    for ep in range(R.epoch + 1):
        seg = {e: [op for op in R.ops[e] if op.epoch == ep and not op.isfence] for e in ENGS}
        fences = {e: [op for op in R.ops[e] if op.epoch == ep and op.isfence] for e in ENGS}
        lasts = [new[e][-1] for e in ENGS if new[e]]
        for e in ENGS:
            for f in fences[e]:
                f.deps = set(l for l in lasts if l.fn is not None)
                f.done, f.fin = True, tnow
                new[e].append(f)
        teng = {e: tnow for e in ENGS}
        for e in ENGS:
            for op in seg[e]:
                op.npend = sum(1 for p in op.preds if not p.done)
                op.ready = max([tnow] + [p.fin + (0.0 if (p.eng == "pe" and e == "pe") else SYNC) for p in op.preds if p.done])
        head = {e: 0 for e in ENGS}
        left = sum(len(v) for v in seg.values())
        groups = {}
        for op in seg["pe"]:
            if op.grp is not None:
                groups.setdefault(op.grp, []).append(op)
        WPE = 1

        def commit(op, st):
            e = op.eng
            op.done = True
            teng[e] = st + op.cost
            op.fin = st + op.cost + op.lat
            new[e].append(op)
            for s_ in op.succ:
                if not s_.done and s_.epoch == ep:
                    s_.npend -= 1
                    r_ = op.fin + (0.0 if (op.eng == "pe" and s_.eng == "pe") else SYNC)
                    if r_ > s_.ready:
                        s_.ready = r_

        while left:
            best = None
            for e in ENGS:
                lst = seg[e]
                h = head[e]
                while h < len(lst) and lst[h].done:
                    h += 1
                head[e] = h
                if h >= len(lst):
                    continue
                cand, cnt, i = None, 0, h
                te = teng[e]
                if e == "pe":
                    seen_g = set()
                    while i < len(lst) and cnt < WPE:
                        op = lst[i]
                        i += 1
                        if op.done:
                            continue
                        g = op.grp
                        if g is not None:
                            if g in seen_g:
                                continue
                            seen_g.add(g)
                            mem = groups[g]
                        else:
                            mem = [op]
                        cnt += 1
                        ok, rd = True, tnow
                        for m in mem:
                            for p in m.preds:
                                if p.grp is not None and p.grp == g:
                                    continue
                                if not p.done:
                                    ok = False
                                    break
                                r_ = p.fin + (0.0 if p.eng == "pe" else SYNC)
                                if r_ > rd:
                                    rd = r_
                            if not ok:
                                break
                        if not ok:
                            continue
                        if rd <= te:
                            cand = (rd, mem)
                            break
                        if cand is None or rd < cand[0]:
                            cand = (rd, mem)
                    if cand is None:
                        continue
                    st = max(te, cand[0])
                    if best is None or st < best[0]:
                        best = (st, cand[1])
                    continue
                while i < len(lst) and cnt < W:
                    op = lst[i]
                    if not op.done:
                        cnt += 1
                        if op.npend == 0:
                            if op.ready <= te:
                                cand = op
                                break
                            if cand is None or op.ready < cand.ready:
                                cand = op
                    i += 1
                if cand is None:
                    continue
                st = max(te, cand.ready)
                if best is None or st < best[0]:
                    best = (st, [cand])
            assert best is not None, "scheduler deadlock"
            st, mem = best
            for m in mem:
                commit(m, max(st, teng[m.eng]))
                left -= 1
        tnow = max([tnow] + [op.fin for e in ENGS for op in seg[e]])
        if os.environ.get('MK_DEBUG'):
            print('EPOCH', ep, round(tnow), {e: round(sum(o.cost for o in seg[e])) for e in ENGS})
    if os.environ.get('MK_DEBUG'):
        import collections as _c
        tg = _c.OrderedDict()
        for e in ENGS:
            for op in new[e]:
                if op.tag is not None:
                    a_, b_ = tg.get(op.tag, (1e18, 0))
                    tg[op.tag] = (min(a_, op.fin - op.cost - op.lat), max(b_, op.fin))
        for k_, (a_, b_) in sorted(tg.items(), key=lambda kv: kv[1][0]):
            print('TAG', k_, round(a_), round(b_))
    for e in ENGS:
        R.ops[e] = new[e]
    return tnow


def build(nl=NL):
    nc = bass.Bass("TRN2", target_bir_lowering=False)
    R = Rec()
    es = ExitStack()

    def din(name, shape, dt=F32):
        return nc.dram_tensor(name, list(shape), dt, kind="ExternalInput").ap()

    def dout(name, shape):
        return nc.dram_tensor(name, list(shape), F32, kind="ExternalOutput").ap()

    xsT = din("xsT", [D, NS]); xpT = din("xpT", [D, 512])
    cfm = din("cfm", [128, 16]); meta = din("meta", [128, 4])
    s0 = din("s0", [2, 8, 128, 128])
    ada_w = din("ada_w", [4, D, 6 * D]); ada_b = din("ada_b", [128, 4 * 48])
    norm_g = din("norm_g", [128, 64]); final_g = din("final_g", [128, 8])
    conv_w_in = din("conv_w_in", [D, 3 * D]); conv_dw = din("conv_dw", [128, 24]); conv_w_out = din("conv_w_out", [D, D])
    pool_w = din("pool_w", [4, 256, 256]); pool_scale = din("pool_scale", [128, 8])
    sgu_w_in = din("sgu_w_in", [D, 2 * D]); sgu_ng = din("sgu_ng", [128, 8]); sgu_wsT = din("sgu_wsT", [8, 128, 128])
    sgu_bs = din("sgu_bs", [1, 1024]); sgu_w_out = din("sgu_w_out", [D, D])
    hg_w_in = din("hg_w_in", [D, 5 * D]); hg_lb = din("hg_lb", [128, 64]); hg_ng = din("hg_ng", [128, 8]); hg_w_out = din("hg_w_out", [D, D])
    ffn_up = din("ffn_up", [4, D, 2 * DFF]); ffn_dw = din("ffn_dw", [128, 4 * 3 * 44]); ffn_down = din("ffn_down", [4, DFF, D])
    ysT = dout("ysT", [D, OWN]); ypT = dout("ypT", [D, 512]); nst = dout("nst", [2, 2, 8, 128, 128])
    xspill = nc.dram_tensor("xspill", [128, 8 * NT], F32).ap()
    hcache = nc.dram_tensor("hcache", [8, 128, 8, 512], BF16).ap()
    cc1_in = nc.dram_tensor("cc1_in", [2048, 128], F32); cc1_out = nc.dram_tensor("cc1_out", [4096, 128], F32)
    cc2_in = nc.dram_tensor("cc2_in", [128, 16], F32); cc2_out = nc.dram_tensor("cc2_out", [256, 16], F32)

    def sb(name, shape, dt=F32):
        return es.enter_context(nc.sbuf_tensor(name, list(shape), dt))

    def ps(name, shape, dt=F32):
        return es.enter_context(nc.psum_tensor(name, list(shape), dt))

    x = sb("x", [128, 8, NT])
    ring = sb("ring", [128, NSLOT, SLOT], BF16)
    hb = sb("hb", [128, 8, 512], BF16)
    actb = sb("actb", [128, 8, 512], BF16)
    NTMP = 6
    tmpf = sb("tmpf", [128, NTMP, 512])
    hb2 = sb("hb2", [128, 8, 512], BF16)
    rmask = sb("rmask", [128, 512])
    sbs = sb("sbs", [128, 8, 128])
    rsb = sb("rsb", [128, 512])
    aux = sb("aux", [128, 4, 512])
    vtm = aux[:].bitcast(BF16)
    _c32 = sb("c32", [128, 2420])
    _c16 = sb("c16", [128, 832], BF16)
    _off = {"32": 0, "16": 0}

    def carve(shape, which="32"):
        n = int(np.prod(shape[1:]))
        base = _c32 if which == "32" else _c16
        v = base[:, _off[which]:_off[which] + n]
        _off[which] += n
        if len(shape) == 3:
            v = v.rearrange("p (a b) -> p a b", a=shape[1])
        elif len(shape) == 4:
            v = v.rearrange("p (a b c) -> p a b c", a=shape[1], b=shape[2])
        elif len(shape) == 5:
            v = v.rearrange("p (a b c d) -> p a b c d", a=shape[1], b=shape[2], c=shape[3])
        return v

    maskt = carve([128, 2 * H], "16")
    ones = carve([128, 128], "16")
    ident = carve([128, 128], "16")
    mtri = carve([128, 2, 128], "16")
    csil = carve([128, 8, 2], "16")
    iot = carve([128, 128])
    iop = carve([128, 1])
    craw = carve([128, 8, 2])
    metat = carve([128, 4])
    modt = carve([128, 4, 48, 2])
    adab = carve([128, 4, 48])
    acoef = carve([128, 4, 2, 8, 2])
    ngt = carve([128, 4, 2, 8])
    fgt = carve([128, 8])
    cdw = carve([128, 3, 8])
    pscl = carve([128, 8]); pgs = carve([128, 8, 2])
    sng = carve([128, 8])
    lbt = carve([128, 4, 2, 8]); lbw = carve([128, 6, 2, 8])
    hng = carve([128, 8])
    fdw = carve([128, 4, 3, 44])
    pet = carve([128, 2, 1, 64])
    petr = carve([128, 4, 40]); petc = carve([128, 4, 64])
    small = carve([128, 64])
    epsb = carve([128, 1])
    one1 = carve([128, 1])
    xhb = carve([128, 2, 8, 8])
    rw = carve([128, 4])

    psum = [ps(f"ps{i}", [128, 512]) for i in range(7)]
    psb = ps("psb", [128, 1024], BF16)
    PK = [("ps", i) for i in range(7)]

    def xk(kc, a, b):
        return [("x", kc, blk) for blk in range(a // 128, (b - 1) // 128 + 1)]

    def xka(a, b):
        out = []
        for kc in range(8):
            out += xk(kc, a, b)
        return out

    _grp = [0]

    def fsz(ap):
        return int(np.prod(ap.shape[1:]))

    def mm(out, lhsT, rhs, start, stop, r, w):
        if start:
            _grp[0] += 1
        op = R.add("pe", lambda e: e.matmul(out, lhsT, rhs, start=start, stop=stop), r, w, cost=max(fsz(out), 64) / 1900.0 + 0.03)
        op.grp = _grp[0]
        return op

    def act(out, in_, func, r, w, scale=1.0, bias=None, accum=None):
        def fn(e):
            kw = {}
            if bias is not None:
                kw["bias"] = bias
            if accum is not None:
                kw["accum_out"] = accum
            return e.activation(out=out, in_=in_, func=func, scale=scale, **kw)
        return R.add("act", fn, r, w, cost=0.25 + fsz(out) / 1400.0)

    def tt(out, in0, in1, op, r, w, eng="dve"):
        return R.add(eng, lambda e: e.tensor_tensor(out=out, in0=in0, in1=in1, op=op), r, w, cost=(0.1 + fsz(out) / 960.0) * (1.5 if eng == 'pool' else 1.0))

    def stt(out, in0, scalar, in1, op0, op1, r, w):
        return R.add("dve", lambda e: e.scalar_tensor_tensor(out=out, in0=in0, scalar=scalar, in1=in1, op0=op0, op1=op1), r, w, cost=0.1 + fsz(out) / 960.0)

    def ts(out, in0, s1, s2, op0, op1, r, w, eng="dve"):
        if s2 is None:
            return R.add(eng, lambda e: e.tensor_scalar(out=out, in0=in0, scalar1=s1, scalar2=None, op0=op0), r, w, cost=0.1 + fsz(out) / 960.0)
        return R.add(eng, lambda e: e.tensor_scalar(out=out, in0=in0, scalar1=s1, scalar2=s2, op0=op0, op1=op1), r, w, cost=0.1 + fsz(out) / 960.0)

    def cp(out, in_, r, w, eng="dve"):
        return R.add(eng, lambda e: e.tensor_copy(out=out, in_=in_), r, w, cost=0.1 + fsz(out) / 960.0)

    def mset(ap, val, w, eng="dve"):
        return R.add(eng, lambda e: e.memset(ap, val), (), w, cost=0.08 + fsz(ap) / 1900.0)

    def rstd_from(rs, src, r, key, dim):
        act(rs, src, AF.Ln, r + ["c0"], [key], scale=1.0 / dim, bias=epsb[:, 0:1])
        act(rs, rs, AF.Exp, [key], [key], scale=-0.5)

    _hb = [0]

    def next_hb():
        _hb[0] += 1
        return (hb, "hb0") if _hb[0] % 2 else (hb2, "hb1")

    def recip(out, in_, r, w):
        return R.add("dve", lambda e: e.reciprocal(out=out, in_=in_), r, w, cost=0.1 + 4 * fsz(out) / 960.0)

    def dma(eng, out, in_, sem, r, w, slow=False):
        if slow:
            return R.add(eng, lambda e: e.dma_start(out=out, in_=in_, allow_slow_non_contiguous=True), r, w, dma=sem, cost=0.15, lat=4.0)
        return R.add(eng, lambda e: e.dma_start(out=out, in_=in_), r, w, dma=sem, cost=0.15, lat=2.0 + fsz(out) * 128 * 4 / 200e3)

    _tmp = [0]

    def tmp():
        i = _tmp[0] % NTMP
        _tmp[0] += 1
        return i, ("tmp", i)

    _slot = [0]
    _nslot = [NSLOT]

    def wload(src_ap, nelem, shape_str=None, **kw):
        s = _slot[0] % _nslot[0]
        _slot[0] += 1
        dst = ring[:, s, 0:nelem]
        if shape_str:
            dst = dst.rearrange(shape_str, **kw)
        dma("pool", dst, src_ap, ("slot", s), (), [("slot", s)])
        return s, dst

    def colblock(W, c0, w):
        return wload(W[:, c0:c0 + w].rearrange("(kc p) c -> p kc c", p=128), 8 * w, "p (kc c) -> p kc c", kc=8)

    _psr = [0]

    def psrot(lo, n):
        i = lo + _psr[0] % n
        _psr[0] += 1
        return i

    _ls = [0]

    def load_small(dst, src, last=False):
        _ls[0] += 1
        dma("sp", dst, src, "ld0", (), ["c0"] if last else [("c0x", _ls[0])])

    load_small(craw[:].rearrange("p a b -> p (a b)"), cfm)
    load_small(metat[:], meta)
    load_small(adab[:].rearrange("p a b -> p (a b)"), ada_b)
    load_small(ngt[:].rearrange("p a b c -> p (a b c)"), norm_g)
    load_small(cdw[:].rearrange("p a b -> p (a b)"), conv_dw)
    load_small(pscl[:], pool_scale)
    load_small(sng[:], sgu_ng)
    load_small(sbs[:].rearrange("p a b -> p (a b)"), sgu_bs.partition_broadcast(128))
    load_small(lbt[:].rearrange("p a b c -> p (a b c)"), hg_lb)
    load_small(hng[:], hg_ng)
    load_small(fdw[:].rearrange("p a b c -> p (a b c)"), ffn_dw)
    for kc in range(8):
        dma("sp", x[:, kc, 0:NS], xsT[kc * 128:(kc + 1) * 128, :], "ld0", (), xk(kc, 0, NS))
        dma("sp", x[:, kc, P0:NT], xpT[kc * 128:(kc + 1) * 128, :], "ld0", (), xk(kc, P0, NT))
    load_small(fgt[:], final_g, last=True)

    C0 = ["c0"]
    mset(ones[:], 1.0, C0)
    mset(epsb[:], EPS, C0)
    mset(one1[:], 1.0, C0)
    R.add("pool", lambda e: e.iota(iot[:], [[1, 128]], base=0, channel_multiplier=0, allow_small_or_imprecise_dtypes=True), (), ["c0i"])
    R.add("pool", lambda e: e.iota(iop[:], [[0, 1]], base=0, channel_multiplier=1, allow_small_or_imprecise_dtypes=True), (), ["c0i"])
    CI = ["c0", "c0i"]
    ts(ident[:], iot[:], iop[:, 0:1], None, ALU.is_equal, None, CI, ["cid"])
    ts(mtri[:, 0, :], iot[:], iop[:, 0:1], None, ALU.is_ge, None, CI, ["cid"])
    ts(mtri[:, 1, :], iot[:], iop[:, 0:1], None, ALU.is_le, None, CI, ["cid"])
    mset(rmask[:], 1.0, ["cid"])
    for q in range(4):
        mset(rmask[:, q * 128:q * 128 + 1], 0.0, ["cid"])
    mset(maskt[:], 1.0, ["cid"])
    ts(maskt[:, 0:H], maskt[:, 0:H], metat[:, 1:2], None, ALU.mult, None, ["c0", "cid"], ["cid"])
    ts(maskt[:, H:2 * H], maskt[:, H:2 * H], metat[:, 2:3], None, ALU.mult, None, ["c0", "cid"], ["cid"])
    C = ["c0", "c0i", "cid"]
    act(csil[:].rearrange("p a b -> p (a b)"), craw[:].rearrange("p a b -> p (a b)"), AF.Silu, C, ["csil"])
    LBE = small[:, 0:64].rearrange("p (a b) -> p a b", a=4)
    act(small[:, 0:64], lbt[:].rearrange("p a b c -> p (a b c)"), AF.Exp, C, ["lb0"])
    lbv = lbw[:].rearrange("p a b c -> p a (b c)")
    tt(lbv[:, 3], LBE[:, 1], LBE[:, 2], ALU.add, ["lb0"], ["lb1"])
    tt(lbv[:, 3], lbv[:, 3], LBE[:, 3], ALU.add, ["lb1"], ["lb1"])
    tt(lbv[:, 4], lbv[:, 3], LBE[:, 0], ALU.add, ["lb0", "lb1"], ["lb2"])
    recip(lbv[:, 4], lbv[:, 4], ["lb2"], ["lb2"])
    tt(lbv[:, 0], lbv[:, 3], lbv[:, 4], ALU.mult, ["lb1", "lb2"], ["lb3"])
    ts(lbv[:, 1], lbv[:, 0], -1.0, 1.0, ALU.mult, ALU.add, ["lb3"], ["lb4"])
    ts(lbv[:, 2], lbv[:, 1], -1.0, None, ALU.mult, None, ["lb4"], ["lb5"])
    LB = ["lb3", "lb4", "lb5"]

    act(pet[:, 0, 0, 0:1], iop[:, 0:1], AF.Exp, CI, ["pe0"], scale=-float(np.log(10000.0)) / 256.0)
    ts(pet[:, 0, 0, 1:2], pet[:, 0, 0, 0:1], float(np.exp(-np.log(10000.0) / 2)), None, ALU.mult, None, ["pe0"], ["pe0"])
    jr = pet[:, 1, 0, 0:40]
    ts(jr, iot[:, 0:40], metat[:, 0:1], None, ALU.add, None, C, ["pe1"])

    def sincos_table(dst, idx_ap, n, m, cosine, key):
        i1, k1 = tmp(); i2, k2 = tmp()
        y = tmpf[:, i1, 0:n]; r_ = tmpf[:, i2, 0:n]
        ts(y, idx_ap, pet[:, 0, 0, m:m + 1], float(1.0 / (2 * np.pi)), ALU.mult, ALU.mult, ["pe0", "pe1"] + C, [k1])
        if cosine:
            ts(y, y, 0.25, None, ALU.add, None, [k1], [k1])
        yi = tmpf[:, i2, 0:n].bitcast(mybir.dt.int32)
        cp(yi, y, [k1], [k2])
        cp(r_, yi, [k2], [k2])
        tt(y, y, r_, ALU.subtract, [k1, k2], [k1])
        ts(r_, y, 0.5, None, ALU.is_gt, None, [k1], [k2])
        tt(y, y, r_, ALU.subtract, [k1, k2], [k1])
        ts(r_, y, -0.5, None, ALU.is_lt, None, [k1], [k2])
        tt(y, y, r_, ALU.add, [k1, k2], [k1])
        act(dst, y, AF.Sin, [k1], [key], scale=6.28318)

    for m in range(2):
        sincos_table(petr[:, m, 0:40], jr, 40, m, False, "petr")
        sincos_table(petr[:, 2 + m, 0:40], jr, 40, m, True, "petr")
        sincos_table(petc[:, m, :], iot[:, 0:64], 64, m, False, "petc")
        sincos_table(petc[:, 2 + m, :], iot[:, 0:64], 64, m, True, "petc")
    NB = (NS - 12) // 64
    for kc in range(4):
        r_, w_ = ["petr"] + xk(kc, 0, NS), xk(kc, 0, NS)
        tt(x[:, kc, 0:12], x[:, kc, 0:12], petr[:, kc, 0:1].to_broadcast([128, 12]), ALU.add, r_, w_)
        for b0 in range(0, NB, 8):
            nb = min(8, NB - b0)
            xa = x[:, kc, 12 + 64 * b0:12 + 64 * (b0 + nb)].rearrange("p (a b) -> p a b", b=64)
            tt(xa, xa, petr[:, kc, 1 + b0:1 + b0 + nb].unsqueeze(2).to_broadcast([128, nb, 64]), ALU.add, r_, w_)
        tt(x[:, kc, NS - 12:NS], x[:, kc, NS - 12:NS], petr[:, kc, 37:38].to_broadcast([128, 12]), ALU.add, r_, w_)
    for kc in range(4, 8):
        r_, w_ = ["petc"] + xk(kc, 0, NS), xk(kc, 0, NS)
        tt(x[:, kc, 0:12], x[:, kc, 0:12], petc[:, kc - 4, 52:64], ALU.add, r_, w_)
        for b0 in range(0, NB, 8):
            nb = min(8, NB - b0)
            xa = x[:, kc, 12 + 64 * b0:12 + 64 * (b0 + nb)].rearrange("p (a b) -> p a b", b=64)
            tt(xa, xa, petc[:, kc - 4, :].unsqueeze(1).to_broadcast([128, nb, 64]), ALU.add, r_, w_)
        tt(x[:, kc, NS - 12:NS], x[:, kc, NS - 12:NS], petc[:, kc - 4, 0:12], ALU.add, r_, w_)

    def adaln(i):
        pb = psrot(6, 1)
        pso = psum[pb][:, 0:96].rearrange("p (a b) -> p a b", b=2)
        for q in range(12):
            s, wv = colblock(ada_w[i], q * 512, 512)
            for nn in range(4):
                n = q * 4 + nn
                for kc in range(8):
                    mm(pso[:, n, :], wv[:, kc, nn * 128:(nn + 1) * 128], csil[:, kc, :], kc == 0, kc == 7,
                       [("slot", s), "csil"], [PK[pb]])
        tt(modt[:, i], pso, adab[:, i, :].unsqueeze(2).to_broadcast([128, 48, 2]), ALU.add, [PK[pb]] + C, [("mod", i)])
        for site in range(2):
            stt(acoef[:, i, site], modt[:, i, 8 + 24 * site:16 + 24 * site, :], 1.0,
                ngt[:, i, site, :].unsqueeze(2).to_broadcast([128, 8, 2]), ALU.add, ALU.mult, [("mod", i)] + C, [("mod", i)])

    def coefs(i, site, which):
        a = lambda kc: acoef[:, i, site, kc, which:which + 1]
        b = lambda kc: modt[:, i, 24 * site + kc, which:which + 1]
        g = lambda kc: modt[:, i, 24 * site + 16 + kc, which:which + 1]
        return a, b, g

    def split(lo, hi, maxw, which, slo, shi):
        n = -(-(hi - lo) // maxw)
        base, rem = divmod(hi - lo, n)
        out, c = [], lo
        for t in range(n):
            w_ = base + (1 if t < rem else 0)
            out.append((c, c + w_, which, slo, shi))
            c += w_
        return out

    def seg_tiles(maxw, slo=0, shi=NS):
        return split(slo, shi, maxw, 0, slo, shi) + [(P0, P0 + 256, 1, P0, P0 + 256), (P1, P1 + 256, 1, P1, P1 + 256)]

    def make_h(i, site, c0, c1, halo, which, slo, shi, mask=True, dst=None, dkey=None, doff=None):
        a0, a1 = max(c0 - halo, slo), min(c1 + halo, shi)
        n, off = a1 - a0, a0 - (c0 - halo)
        ntot = c1 - c0 + 2 * halo
        a, b, g = coefs(i, site, which)
        if dst is None:
            dst, dkey = next_hb()
        if doff is not None:
            off = doff
        DK = dkey if isinstance(dkey, list) else [dkey]
        MK = [("mod", i)]
        for kc in range(8):
            act(dst[:, kc, off:off + n], x[:, kc, a0:a1], AF.Square, xk(kc, a0, a1), DK)
        pb = psrot(6, 1)
        for kc in range(8):
            mm(psum[pb][:, 0:n], ones[:], dst[:, kc, off:off + n], kc == 0, kc == 7, DK + ["c0"], [PK[pb]])
        k1 = "rsb"
        rs = rsb[:, 0:n]
        rstd_from(rs, psum[pb][:, 0:n], [PK[pb]], k1, D)
        for kc in range(8):
            i2, k2 = tmp()
            xr = tmpf[:, i2, 0:n]
            tt(xr, x[:, kc, a0:a1], rs, ALU.mult, xk(kc, a0, a1) + [k1], [k2])
            act(dst[:, kc, off:off + n], xr, AF.Identity, [k2] + MK, DK, scale=a(kc), bias=b(kc))
        if mask and which == 0:
            for (m0, m1, mo) in ((0, H, 0), (OWN1, NS, H)):
                lo_, hi_ = max(a0, m0), min(a1, m1)
                if lo_ < hi_:
                    v = dst[:, :, off + lo_ - a0:off + hi_ - a0]
                    tt(v, v, maskt[:, mo + lo_ - m0:mo + hi_ - m0].unsqueeze(1).to_broadcast([128, 8, hi_ - lo_]),
                       ALU.mult, DK + ["cid"], DK)
        if doff is None and off > 0:
            mset(dst[:, :, 0:off], 0.0, DK)
        if doff is None and off + n < ntot:
            mset(dst[:, :, off + n:ntot], 0.0, DK)
        return ntot, dst, dkey

    def dwconv_from_psum(pt, pk, w, wcol, ykey_i):
        iy, ky = ykey_i
        y = tmpf[:, iy, 0:w]
        act(y, pt[:, 1:1 + w], AF.Identity, [pk, "c0"], [ky], scale=wcol(1))
        stt(y, pt[:, 0:w], wcol(0), y, ALU.mult, ALU.add, [pk, ky, "c0"], [ky])
        stt(y, pt[:, 2:2 + w], wcol(2), y, ALU.mult, ALU.add, [pk, ky, "c0"], [ky])
        return y

    def resid_add(i, site, which, m, pt, pk, c0, c1, extra_scale=None, defer=None):
        a, b, g = coefs(i, site, which)
        sc = g(m) if extra_scale is None else extra_scale(m)
        rk = [pk, ("mod", i), "pgs"]
        if defer is None:
            stt(x[:, m, c0:c1], pt, sc, x[:, m, c0:c1], ALU.mult, ALU.add, rk + xk(m, c0, c1), xk(m, c0, c1))
        else:
            d_, sl_ = defer
            w_ = c1 - c0
            stt(x[:, m, c0:c1 - d_], pt[:, 0:w_ - d_], sc, x[:, m, c0:c1 - d_], ALU.mult, ALU.add, rk + xk(m, c0, c1 - d_), xk(m, c0, c1 - d_))
            stt(xhb[:, sl_, m, 0:d_], pt[:, w_ - d_:w_], sc, x[:, m, c1 - d_:c1], ALU.mult, ALU.add, rk + xk(m, c1 - d_, c1), [("xh", sl_, m)])

    def flush(pend):
        if pend is None:
            return
        c1, d_, sl_ = pend
        cp(x[:, :, c1 - d_:c1], xhb[:, sl_, :, 0:d_], [("xh", sl_, m) for m in range(8)], xka(c1 - d_, c1))

    def adjacent(tiles, idx):
        return idx + 1 < len(tiles) and tiles[idx + 1][0] == tiles[idx][1] and tiles[idx + 1][3:5] == tiles[idx][3:5]

    def ffn(i, tiles):
        for grp in range(3):
            pieces = []
            for q in (2 * grp, 2 * grp + 1):
                wq = 512 if q < 5 else 256
                sa, va = colblock(ffn_up[i], 512 * q, wq)
                sb_, vb = colblock(ffn_up[i], DFF + 512 * q, wq)
                sd, vd = wload(ffn_down[i][512 * q:512 * q + wq, :].rearrange("(jj p) n -> p jj n", p=128),
                               (wq // 128) * 1024, "p (jj n) -> p jj n", n=1024)
                pieces.append((wq // 128, sa, va, sb_, vb, sd, vd))
            chunks = []
            for pi_, (nj, sa, va, sb_, vb, sd, vd) in enumerate(pieces):
                for jj in range(nj):
                    chunks.append((sa, va, sb_, vb, sd, vd, jj, 4 * (2 * grp + pi_) + jj))
            nck = len(chunks)

            def setup_h(ti):
                (c0, c1, which, slo, shi) = tiles[ti]
                n = c1 - c0 + 2
                if grp == 0:
                    _, hq, hqk = make_h(i, 1, c0, c1, 1, which, slo, shi)
                    dma("sp", hcache[ti, :, :, 0:n], hq[:, :, 0:n], ("hcst", hqk), [hqk], [("hc", ti)])
                else:
                    hq, hqk = next_hb()
                    dma("sp", hq[:, :, 0:n], hcache[ti, :, :, 0:n], ("hcld", hqk), [("hc", ti)], [hqk])
                return hq, hqk, n

            def chunk_mm(hs, ci):
                hq, hqk, n = hs
                (sa, va, sb_, vb, sd, vd, jj, j) = chunks[ci]
                pa = psrot(0, 4); pbk = psrot(0, 4)
                for kc in range(8):
                    mm(psum[pa][:, 0:n], va[:, kc, jj * 128:(jj + 1) * 128], hq[:, kc, 0:n], kc == 0, kc == 7,
                       [("slot", sa), hqk], [PK[pa]])
                for kc in range(8):
                    mm(psum[pbk][:, 0:n], vb[:, kc, jj * 128:(jj + 1) * 128], hq[:, kc, 0:n], kc == 0, kc == 7,
                       [("slot", sb_), hqk], [PK[pbk]])
                return pa, pbk

            def chunk_ew(ci, w, pa, pbk):
                j = chunks[ci][7]
                ya = dwconv_from_psum(psum[pa], PK[pa], w, lambda t, j=j: fdw[:, i, t, j:j + 1], tmp())
                kya = ("tmp", (_tmp[0] - 1) % NTMP)
                yb = dwconv_from_psum(psum[pbk], PK[pbk], w, lambda t, j=j: fdw[:, i, t, 22 + j:23 + j], tmp())
                kyb = ("tmp", (_tmp[0] - 1) % NTMP)
                act(ya, ya, AF.Silu, [kya], [kya])
                tt(actb[:, ci, 0:w], ya, yb, ALU.mult, [kya, kyb], [("actb", ci)])

            hs = setup_h(0)
            pre = {}
            for ti, (c0, c1, which, slo, shi) in enumerate(tiles):
                w = c1 - c0
                hs_next = None
                for ci in range(nck):
                    banks = pre.pop(ci) if ci in pre else chunk_mm(hs, ci)
                    chunk_ew(ci, w, *banks)
                    if ci == min(3, nck - 1) and ti + 1 < len(tiles):
                        hs_next = setup_h(ti + 1)
                if hs_next is not None:
                    for ci in range(2):
                        pre[ci] = chunk_mm(hs_next, ci)
                for m in range(8):
                    py = psrot(4, 2)
                    for ci in range(nck):
                        (sa, va, sb_, vb, sd, vd, jj, j) = chunks[ci]
                        mm(psum[py][:, 0:w], vd[:, jj, m * 128:(m + 1) * 128], actb[:, ci, 0:w], ci == 0, ci == nck - 1,
                           [("slot", sd), ("actb", ci)], [PK[py]])
                    resid_add(i, 1, which, m, psum[py][:, 0:w], PK[py], c0, c1)
                hs = hs_next

    def mixer_conv(i, tiles):
        for grp in range(2):
            sl = [colblock(conv_w_in, 1024 * bi + 512 * grp, 512) for bi in range(3)]
            so_, vo = wload(conv_w_out[512 * grp:512 * grp + 512, :].rearrange("(jj p) n -> p jj n", p=128), 4096, "p (jj n) -> p jj n", n=1024)

            def setup_h(ti):
                (c0, c1, which, slo, shi) = tiles[ti]
                n = c1 - c0 + 4
                if grp == 0:
                    _, hq, hqk = make_h(i, 0, c0, c1, 2, which, slo, shi)
                    dma("sp", hcache[ti, :, :, 0:n], hq[:, :, 0:n], ("hcst", hqk), [hqk], [("hc", ti)])
                else:
                    hq, hqk = next_hb()
                    dma("sp", hq[:, :, 0:n], hcache[ti, :, :, 0:n], ("hcld", hqk), [("hc", ti)], [hqk])
                return hq, hqk, n

            def chunk_mm(hs, jj):
                hq, hqk, n = hs
                pp = [psrot(0, 4) for _ in range(3)]
                for bi in range(3):
                    s_, v = sl[bi]
                    for kc in range(8):
                        mm(psum[pp[bi]][:, 0:n], v[:, kc, jj * 128:(jj + 1) * 128], hq[:, kc, 0:n], kc == 0, kc == 7,
                           [("slot", s_), hqk], [PK[pp[bi]]])
                return pp

            def chunk_ew(jj, w, pp):
                j = 4 * grp + jj
                nm = w + 2
                ic, kcg = tmp(); im, km = tmp()
                cgs = tmpf[:, ic, 0:nm]; mt = tmpf[:, im, 0:nm]
                act(cgs, psum[pp[1]][:, 1:1 + nm], AF.Copy, [PK[pp[1]]], [kcg])
                tt(mt, cgs, psum[pp[2]][:, 1:1 + nm], ALU.mult, [kcg, PK[pp[2]]], [km])
                iy, ky = tmp()
                y = tmpf[:, iy, 0:w]
                act(y, mt[:, 1:1 + w], AF.Identity, [km, "c0"], [ky], scale=cdw[:, 1, j:j + 1])
                stt(y, mt[:, 0:w], cdw[:, 0, j:j + 1], y, ALU.mult, ALU.add, [km, ky, "c0"], [ky])
                stt(y, mt[:, 2:2 + w], cdw[:, 2, j:j + 1], y, ALU.mult, ALU.add, [km, ky, "c0"], [ky])
                tt(actb[:, jj, 0:w], y, psum[pp[0]][:, 2:2 + w], ALU.mult, [ky, PK[pp[0]]], [("actb", jj)])

            hs = setup_h(0)
            pre = {}
            for ti, (c0, c1, which, slo, shi) in enumerate(tiles):
                w = c1 - c0
                hs_next = None
                for jj in range(4):
                    pp = pre.pop(jj) if jj in pre else chunk_mm(hs, jj)
                    chunk_ew(jj, w, pp)
                    if jj == 1 and ti + 1 < len(tiles):
                        hs_next = setup_h(ti + 1)
                if hs_next is not None:
                    pre[0] = chunk_mm(hs_next, 0)
                for m in range(8):
                    py = psrot(4, 2)
                    for jj in range(4):
                        mm(psum[py][:, 0:w], vo[:, jj, m * 128:(m + 1) * 128], actb[:, jj, 0:w], jj == 0, jj == 3,
                           [("slot", so_), ("actb", jj)], [PK[py]])
                    resid_add(i, 0, which, m, psum[py][:, 0:w], PK[py], c0, c1)
                hs = hs_next

    def mixer_pool(i, tiles):
        s, wv = wload(pool_w.rearrange("g (kc p) n -> p g kc n", p=128), 2048, "p (g kc n) -> p g kc n", g=4, kc=2)
        tt(pgs[:], modt[:, i, 16:24, :], pscl[:].unsqueeze(2).to_broadcast([128, 8, 2]), ALU.mult, [("mod", i)] + C, ["pgs"])
        for g_ in range(4):
            mset(rw[:, g_:g_ + 1], 1.0 / (2 << g_), ["rw"])
        HL = 8
        pend = None

        def levels(src, ksrc, n, grp):
            i_a, k_a = tmp()
            s_ = tmpf[:, i_a, 0:n]
            tt(s_[:, 1:n], src[:, 0:n - 1], src[:, 1:n], ALU.add, [ksrc], [k_a])
            d = 1
            for lev in range(grp):
                i_b, k_b = tmp()
                s2 = tmpf[:, i_b, 0:n]
                tt(s2[:, d:n - d], s_[:, 0:n - 2 * d], s_[:, 2 * d:n], ALU.add, [k_a], [k_b])
                s_, k_a, d = s2, k_b, d * 2
            return s_, k_a

        for ti, (c0, c1, which, slo, shi) in enumerate(tiles):
            dfr = (HL, ti % 2) if adjacent(tiles, ti) else None
            w = c1 - c0
            n = w + 2 * HL
            a0, a1 = max(c0 - HL, slo), min(c1 + HL, shi)
            off, nv = a0 - (c0 - HL), a1 - a0
            interior = (which == 0 and a0 >= H and a1 <= OWN1 and nv == n)
            kmk = "aux0"
            mk = aux[:, 0, 0:n]
            if not interior:
                mset(mk, 0.0, [kmk])
                mset(mk[:, off:off + nv], 1.0, [kmk])
                if which == 0:
                    for (m0, m1, mo) in ((0, H, 0), (OWN1, NS, H)):
                        lo_, hi_ = max(a0, m0), min(a1, m1)
                        if lo_ < hi_:
                            cp(mk[:, off + lo_ - a0:off + hi_ - a0], maskt[:, mo + lo_ - m0:mo + hi_ - m0], [kmk, "cid"], [kmk])
            hq, hqk = next_hb()
            pb = psrot(6, 1)
            for kc in range(8):
                act(hq[:, kc, 0:nv], x[:, kc, a0:a1], AF.Square, xk(kc, a0, a1), [hqk])
            for kc in range(8):
                mm(psum[pb][:, 0:nv], ones[:], hq[:, kc, 0:nv], kc == 0, kc == 7, [hqk, "c0"], [PK[pb]])
            krs = "rsb"
            rs = rsb[:, 0:nv]
            rstd_from(rs, psum[pb][:, 0:nv], [PK[pb]], krs, D)
            a, b, g = coefs(i, 0, which)
            for grp in range(4):
                kcn = "aux2"
                cn = aux[:, 2, 0:n]
                if not interior:
                    s_, k_a = levels(mk, kmk, n, grp)
                    ts(cn, s_, 1.0, None, ALU.max, None, [k_a], [kcn])
                    recip(cn, cn, [kcn], [kcn])
                for kk in range(2):
                    kc = 2 * grp + kk
                    kh = ("aux3", kk)
                    hfp = aux[:, 1 if kk else 3, 0:n]
                    if not interior:
                        mset(hfp, 0.0, [kh])
                    ix, kx = tmp()
                    xr = tmpf[:, ix, 0:nv]
                    tt(xr, x[:, kc, a0:a1], rs, ALU.mult, xk(kc, a0, a1) + [krs], [kx])
                    act(hfp[:, off:off + nv], xr, AF.Identity, [kx, ("mod", i)], [kh], scale=a(kc), bias=b(kc))
                    if not interior:
                        tt(hfp, hfp, mk, ALU.mult, [kh, kmk], [kh])
                    s_, k_a = levels(hfp, kh, n, grp)
                    if interior:
                        stt(actb[:, kc, 0:w], s_[:, HL:HL + w], rw[:, grp:grp + 1], hfp[:, HL:HL + w], ALU.mult, ALU.subtract,
                            [k_a, kh, "rw"], [("actb", kc)])
                    else:
                        tt(s_, s_, cn, ALU.mult, [k_a, kcn], [k_a])
                        tt(actb[:, kc, 0:w], s_[:, HL:HL + w], hfp[:, HL:HL + w], ALU.subtract, [k_a, kh], [("actb", kc)])
                for nn in range(2):
                    py = psrot(4, 2)
                    for kk in range(2):
                        mm(psum[py][:, 0:w], wv[:, grp, kk, nn * 128:(nn + 1) * 128], actb[:, 2 * grp + kk, 0:w], kk == 0, kk == 1,
                           [("slot", s), ("actb", 2 * grp + kk)], [PK[py]])
                    m = 2 * grp + nn
                    resid_add(i, 0, which, m, psum[py][:, 0:w], PK[py], c0, c1, extra_scale=lambda m_, wh=which: pgs[:, m_, wh:wh + 1], defer=dfr)
            flush(pend)
            pend = (c1, HL, ti % 2) if dfr else None

    def mixer_sgu(i, tiles):
        sl = [colblock(sgu_w_in, 512 * q, 512) for q in range(4)]
        so = [colblock(sgu_w_out, 512 * q, 512) for q in range(2)]
        sw, wsv = wload(sgu_wsT.rearrange("g q p -> q g p"), 1024, "q (g p) -> q g p", g=8)
        def sgu_h(ti):
            (c0_, c1_, wh_, slo_, shi_) = tiles[ti]
            _, hq_, hqk_ = make_h(i, 0, c0_, c1_, 0, wh_, slo_, shi_, mask=False)
            return hq_, hqk_

        hnext = sgu_h(0)
        for ti, (c0, c1, which, slo, shi) in enumerate(tiles):
            w = c1 - c0
            nch = w // 128
            hq, hqk = hnext
            for j in range(8):
                pu = psrot(0, 4)
                s, v = sl[j // 4]
                for kc in range(8):
                    mm(psum[pu][:, 0:w], v[:, kc, (j % 4) * 128:(j % 4 + 1) * 128], hq[:, kc, 0:w], kc == 0, kc == 7,
                       [("slot", s), hqk], [PK[pu]])
                act(actb[:, j, 0:w], psum[pu][:, 0:w], AF.Gelu_apprx_tanh, [PK[pu]], [("actb", j)])
                if j == 3 and ti + 1 < len(tiles):
                    hnext = sgu_h(ti + 1)
            for q in range(nch):
                iv0, kv0 = tmp(); iv1, kv1 = tmp()
                vts = [tmpf[:, iv0, :], tmpf[:, iv1, :]]
                kvs = [kv0, kv1]
                iss, kss = tmp()
                ss = tmpf[:, iss, 0:4]
                for half in range(2):
                    pv = psrot(0, 4)
                    s, v = sl[2 + half]
                    for kc in range(8):
                        mm(psum[pv][:, :], hq[:, kc, q * 128:(q + 1) * 128], v[:, kc, :], kc == 0, kc == 7,
                           [("slot", s), hqk], [PK[pv]])
                    act(vts[half], psum[pv][:, :], AF.Gelu_apprx_tanh, [PK[pv]], [kvs[half]])
                    ij, kj = tmp()
                    act(tmpf[:, ij, :], vts[half], AF.Square, [kvs[half]], [kj, kss], accum=ss[:, half:half + 1])
                tt(ss[:, 2:3], ss[:, 0:1], ss[:, 1:2], ALU.add, [kss], [kss])
                rstd_from(ss[:, 3:4], ss[:, 2:3], [kss], kss, D)
                for half in range(2):
                    ts(vtm[:, q, half * 512:(half + 1) * 512], vts[half], ss[:, 3:4], None, ALU.mult, None, [kvs[half], kss], [("vtm", q)])
            for gi in range(8):
                pg = psrot(0, 4)
                for q in range(nch):
                    mm(psum[pg][:, q * 128:(q + 1) * 128], vtm[:, q, gi * 128:(gi + 1) * 128], wsv[:, gi, :], True, True,
                       [("slot", sw), ("vtm", q)], [PK[pg]])
                isg, ksg = tmp()
                sg_ = tmpf[:, isg, 0:w]
                stt(sg_.rearrange("p (a b) -> p a b", b=128), psum[pg][:, 0:w].rearrange("p (a b) -> p a b", b=128), sng[:, gi:gi + 1],
                    sbs[:, gi, :].unsqueeze(1).to_broadcast([128, nch, 128]), ALU.mult, ALU.add, [PK[pg]] + C, [ksg])
                tt(actb[:, gi, 0:w], actb[:, gi, 0:w], sg_, ALU.mult, [ksg, ("actb", gi)], [("actb", gi)])
            for m in range(8):
                py = psrot(4, 2)
                s, v = so[m // 4]
                for j in range(8):
                    mm(psum[py][:, 0:w], v[:, j, (m % 4) * 128:(m % 4 + 1) * 128], actb[:, j, 0:w], j == 0, j == 7,
                       [("slot", s), ("actb", j)], [PK[py]])
                resid_add(i, 0, which, m, psum[py][:, 0:w], PK[py], c0, c1)


    def mixer_hgrn(i):
        actf = actb[:].rearrange("p a b -> p (a b)").bitcast(F32).rearrange("p (a b) -> p a b", a=4)
        LT = [(tmpf[:, j_, :], ("tmp", j_)) for j_ in range(NTMP)] + [(aux[:, j_, :], ("auxL", j_)) for j_ in range(2, 4)] + \
             [(actf[:, j_, :], ("actL", j_)) for j_ in range(3)]
        _lt = [0]

        def tmpL():
            v = LT[_lt[0] % len(LT)]
            _lt[0] += 1
            return v

        xflat = x[:].rearrange("p a b -> p (a b)")
        xb16 = xflat.bitcast(BF16)
        QT = [xb16[:, 2560 * d:2560 * (d + 1)] for d in range(2)]
        KT = [xb16[:, 5120 + 2560 * d:5120 + 2560 * (d + 1)] for d in range(2)]
        KH = [xb16[:, 10240 + 2560 * d:10240 + 2560 * (d + 1)].rearrange("p (n c) -> p n c", c=128) for d in range(2)]
        VT = xb16[:, 15360:17920].rearrange("p (n c) -> p n c", c=128)
        SE = [xb16[:, 17920 + 2560 * d:17920 + 2560 * (d + 1)].rearrange("p (n c) -> p n c", c=128) for d in range(2)]
        S32p = [xflat[:, 21760:22016].rearrange("p (d c) -> p d c", d=2), actf[:, 3, 0:256].rearrange("p (d c) -> p d c", d=2)]
        Dt = xflat[:, 22016:22056].rearrange("p (d c) -> p d c", d=2)
        s0t = xflat[:, 22060:22316].rearrange("p (d c) -> p d c", d=2)
        Gt = xflat[:, 22320:22576].rearrange("p (d c) -> p d c", d=2)
        h3r = ring[:, 2:7, :].rearrange("p s e -> p (s e)").rearrange("p (kc c) -> p kc c", kc=8)
        h3 = xb16[:, 23040:43520].rearrange("p (kc c) -> p kc c", kc=8)
        HKR = [("slot", s_) for s_ in range(2, 7)]
        HK = ["h3"]
        T3 = [(OWN0 + 512 * t, OWN0 + 512 * (t + 1), 0, 512 * t) for t in range(4)] + [(P0, P0 + 256, 1, 2048), (P1, P1 + 256, 1, 2304)]
        SEQ = [(0, 16, "S"), (16, 2, 0), (18, 2, 1)]
        for (c0, c1, which, t0) in T3:
            make_h(i, 0, c0, c1, 0, which, c0, c1, mask=False, dst=h3r, dkey=HKR, doff=t0)
        for kc in range(8):
            dma("sp", xspill[:, kc * NT:(kc + 1) * NT], x[:, kc, :], "spill", xk(kc, 0, NT), ["xspill"])
        R.fence()
        for kc in range(8):
            if kc % 2:
                act(h3[:, kc, :], h3r[:, kc, :], AF.Copy, HKR, HK)
            else:
                cp(h3[:, kc, :], h3r[:, kc, :], HKR, HK)

        TIDX = {}
        for ti_, (c0_, c1_, wh_, t0_) in enumerate(T3):
            for n_ in range(t0_ // 128, (t0_ + c1_ - c0_) // 128):
                TIDX[n_] = ti_
        hbufs = [hb, hb2]

        def prep_gen(hd, wv, sw, tiles, full):
            pendB = []

            def flushB(keep):
                while len(pendB) > keep:
                    pendB.pop(0)()

            for (c0, c1, which, t0) in tiles:
                w = c1 - c0
                nch = w // 128
                ch0 = t0 // 128
                tix = TIDX[ch0]
                par = tix % 2
                WK = [("slot", sw)] + HK
                pv = psrot(0, 4)
                for q in range(nch):
                    for kc in range(8):
                        mm(psum[pv][:, q * 128:(q + 1) * 128], h3[:, kc, t0 + q * 128:t0 + (q + 1) * 128], wv[:, kc, 3, :], kc == 0, kc == 7, WK, [PK[pv]])
                act(VT[:, ch0:ch0 + nch, :], psum[pv][:, 0:w].rearrange("p (n c) -> p n c", c=128), AF.Copy, [PK[pv]],
                    [("VT", n_) for n_ in range(ch0, ch0 + nch)])
                pq = None
                if full:
                    pq = psrot(4, 2)
                    for kc in range(8):
                        mm(psum[pq][:, 0:w], wv[:, kc, 0, :], h3[:, kc, t0:t0 + w], kc == 0, kc == 7, WK, [PK[pq]])
                for d in range(2):
                    pz = psrot(0, 4)
                    for kc in range(8):
                        mm(psum[pz][:, 0:w], wv[:, kc, 1 + d, :], h3[:, kc, t0:t0 + w], kc == 0, kc == 7, WK, [PK[pz]])
                    ia, ka = tmpL(); ib, kb = tmpL(); ic2, kc2 = tmpL(); ic, kc_ = tmpL()
                    A = ia[:, 0:w]; B = ib[:, 0:w]; C2 = ic2[:, 0:w]; Cc = ic[:, 0:w]
                    lb0 = lbw[:, 0, d, hd:hd + 1]; lb1 = lbw[:, 1, d, hd:hd + 1]
                    act(A, psum[pz][:, 0:w], AF.Exp, [PK[pz]], [ka], scale=-1.0)
                    act(B, A, AF.Ln, [ka, "c0"] + LB, [kb], scale=lb0, bias=one1[:, 0:1])
                    act(C2, A, AF.Ln, [ka, "c0"], [kc2], bias=one1[:, 0:1])
                    tt(B, B, C2, ALU.subtract, [kb, kc2], [kb])
                    act(C2, C2, AF.Exp, [kc2], [kc2], scale=-1.0)
                    stt(A, A, lb1, C2, ALU.mult, ALU.mult, [ka, kc2] + LB, [ka])
                    R.add("dve", lambda e, Cc=Cc, B=B, w=w: e.tensor_tensor_scan(out=Cc, data0=rmask[:, 0:w], data1=B, initial=0.0,
                                                                                op0=ALU.mult, op1=ALU.add), [kb, "cid"], [kc_], cost=0.1 + 2 * w / 960.0)
                    C3 = Cc.rearrange("p (n c) -> p n c", c=128)
                    if d == 0:
                        bc, kbc = Cc, kc_
                        last = C3[:, :, 127:128]
                    else:
                        tt(B, B, Cc, ALU.subtract, [kb, kc_], [kb])
                        B3 = B.rearrange("p (n c) -> p n c", c=128)
                        tt(B3, B3, C3[:, :, 127:128].to_broadcast([128, nch, 128]), ALU.add, [kb, kc_], [kb])
                        bc, kbc = B, kb
                        last = B3[:, :, 0:1]
                    act(Dt[:, d, ch0:ch0 + nch].unsqueeze(2), last, AF.Exp, [kbc], [("Dt", d, tix)])
                    if full:
                        E = C2
                        act(E, bc, AF.Exp, [kbc, kc2], [kc2])
                        tt(QT[d][:, t0:t0 + w], E, psum[pq][:, 0:w], ALU.mult, [kc2, PK[pq]], [("QT", d, tix)])
                    ien, ken = tmpL()
                    En = ien[:, 0:w]
                    act(En, bc, AF.Exp, [kbc], [ken], scale=-1.0)
                    tt(En, A, En, ALU.mult, [ka, ken], [ken])
                    if full:
                        act(KT[d][:, t0:t0 + w], En, AF.Copy, [ken], [("KT", d, tix)])
                    kh = hbufs[par][:, 4 + d, 0:w]
                    tt(kh.rearrange("p (n c) -> p n c", c=128), En.rearrange("p (n c) -> p n c", c=128),
                       Dt[:, d, ch0:ch0 + nch].unsqueeze(2).to_broadcast([128, nch, 128]), ALU.mult, [ken, ("Dt", d, tix)], [("hbk", d, par)])
                    def stageB(d=d, par=par, kh=kh, nch=nch, ch0=ch0, w=w):
                        for q in range(nch):
                            R.add("pe", lambda e, q=q, kh=kh, d=d: e.transpose(psb[:, d * 512 + q * 128:d * 512 + (q + 1) * 128], kh[:, q * 128:(q + 1) * 128], ident[:]),
                                  [("hbk", d, par), "cid"], [("psb", d)], cost=0.12)
                        act(KH[d][:, ch0:ch0 + nch, :], psb[:, d * 512:d * 512 + w].rearrange("p (n c) -> p n c", c=128), AF.Copy, [("psb", d)],
                            [("KH", d, n_) for n_ in range(ch0, ch0 + nch)])

                    pendB.append(stageB)
                    flushB(int(os.environ.get('MK_LAG', '2')))
                yield
            flushB(0)

        def prep(hd, wv, sw, tiles, full):
            for _ in prep_gen(hd, wv, sw, tiles, full):
                pass

        _pp = {0: 0, 1: 0}

        def s32cur(d):
            return S32p[_pp[d]][:, d, :], ("S32", d, _pp[d])

        def sweep(hd, d, c_first, nchk, store):
            order = range(c_first, c_first + nchk) if d == 0 else range(c_first + nchk - 1, c_first - 1, -1)
            for n in order:
                cur, kcur = s32cur(d)
                if store:
                    act(SE[d][:, n, :], cur, AF.Copy, [kcur], [("SE", d, n)])
                pu = psrot(0, 6)
                mm(psum[pu][:, 0:128], KH[d][:, n, :], VT[:, n, :], True, True, [("KH", d, n), ("VT", n)], [PK[pu]])
                _pp[d] ^= 1
                nxt, knxt = s32cur(d)
                stt(nxt, cur, Dt[:, d, n:n + 1], psum[pu][:, 0:128], ALU.mult, ALU.add,
                    [kcur, ("Dt", d, TIDX[n]), PK[pu]], [knxt])

        def head_piece(hd):
            sl_ = _slot[0] % _nslot[0]
            _slot[0] += 1
            view = ring[:, sl_, 0:4096].rearrange("p (kc s c) -> p kc s c", kc=8, s=4)
            for sblk in range(4):
                c0_ = sblk * 1024 + hd * 128
                dma("pool", view[:, :, sblk, :], hg_w_in[:, c0_:c0_ + 128].rearrange("(kc p) c -> p kc c", p=128),
                    ("slot", sl_), (), [("slot", sl_)])
            return sl_, view

        for hd in range(8):
            R.tag = ('P1prep', hd)
            sw, wv = head_piece(hd)
            dma("sp", s0t[:, 0, :], s0[0, hd], ("s0ld", 0), [], [("s0t", 0)])
            dma("sp", s0t[:, 1, :], s0[1, hd], ("s0ld", 1), [], [("s0t", 1)])
            prep(hd, wv, sw, T3[0:4], False)
            R.tag = ('P1sweep', hd)
            for d in range(2):
                cur, kcur = s32cur(d)
                cp(cur, s0t[:, d, :], [("s0t", d)], [kcur])
                sweep(hd, d, 0, 16, False)
                cur, kcur = s32cur(d)
                dma("sp", cc1_in.ap()[(d * 8 + hd) * 128:(d * 8 + hd + 1) * 128, :], cur, ("cc1st", d), [kcur], [("cc1in", hd, d)])
        allin = [("cc1in", hd, d) for hd in range(8) for d in range(2)]
        R.add("pool", lambda e: e.collective_compute("AllGather", ALU.bypass, replica_groups=[[0, 1], [2, 3], [4, 5], [6, 7]],
                                                    ins=[cc1_in.ap().opt()], outs=[cc1_out.ap().opt()]), allin, ["cc1out"], dma="cc1", inc=1, cost=1.0, lat=40.0)
        ogd = nc.dram_tensor("ogd", [8, 128, 2560], BF16).ap()
        HW = {}

        def start_head(hd):
            R.tag = ('P2prep', hd)
            sw, wv = head_piece(hd)
            sg_, wg = colblock(hg_w_in, 4096 + hd * 128, 128)
            HW[hd] = (sg_, wg)
            dma("sp", s0t[:, 0, :], s0[0, hd], ("s0ld", 0), [], [("s0t", 0)])
            dma("sp", s0t[:, 1, :], s0[1, hd], ("s0ld", 1), [], [("s0t", 1)])
            dma("sp", Gt[:, 0, :], cc1_out.ap()[hd * 128:(hd + 1) * 128, :], ("gld", 0), ["cc1out"], [("Gt", 0)])
            dma("sp", Gt[:, 1, :], cc1_out.ap()[2048 + (8 + hd) * 128:2048 + (9 + hd) * 128, :], ("gld", 1), ["cc1out"], [("Gt", 1)])
            return prep_gen(hd, wv, sw, T3, True)

        gen0 = start_head(0)
        for _ in gen0:
            pass
        for hd in range(8):
            sg_, wg = HW[hd]
            R.tag = ('P2sweep', hd)
            for (c_first, nchk, kind) in SEQ:
                for d in range(2):
                    cur, kcur = s32cur(d)
                    if kind == "S":
                        stt(cur, Gt[:, d, :], metat[:, 1 + d:2 + d], s0t[:, d, :], ALU.mult, ALU.add,
                            [("Gt", d), ("s0t", d), "c0"], [kcur])
                    else:
                        mset(cur, 0.0, [kcur])
                    sweep(hd, d, c_first, nchk, True)
                    if kind != "S":
                        cur, kcur = s32cur(d)
                        dma("sp", nst[kind, d, hd], cur, ("nst", d, _pp[d]), [kcur], [])
            R.tag = ('P2out', hd)
            st_ = {}

            def S1(ti_):
                (c0, c1, which, t0) = T3[ti_]
                w = c1 - c0
                nch = w // 128
                ch0 = t0 // 128
                tix = TIDX[ch0]
                par = tix % 2
                hbq = hbufs[par]
                sc = [hbq[:, d, 0:w] for d in range(2)]
                for d in range(2):
                    pscore = psrot(0, 4)
                    for q in range(nch):
                        cs = slice(t0 + q * 128, t0 + (q + 1) * 128)
                        mm(psum[pscore][:, q * 128:(q + 1) * 128], KT[d][:, cs], QT[d][:, cs], True, True, [("KT", d, tix), ("QT", d, tix)], [PK[pscore]])
                    tt(sc[d].rearrange("p (n c) -> p n c", c=128), psum[pscore][:, 0:w].rearrange("p (n c) -> p n c", c=128),
                       mtri[:, d, :].unsqueeze(1).to_broadcast([128, nch, 128]), ALU.mult, [PK[pscore], "cid"], [("hbs", d, par)])
                pg = psrot(0, 4)
                for kc in range(8):
                    mm(psum[pg][:, 0:w], wg[:, kc, :], h3[:, kc, t0:t0 + w], kc == 0, kc == 7, [("slot", sg_)] + HK, [PK[pg]])
                ksl = ("hsl", par)
                sl_ = hbq[:, 6:8, :].rearrange("p a b -> p (a b)").bitcast(F32)[:, 0:w]
                act(sl_, psum[pg][:, 0:w], AF.Exp, [PK[pg]], [ksl], scale=-1.0)
                act(sl_, sl_, AF.Ln, [ksl, "c0"], [ksl], bias=one1[:, 0:1])
                act(sl_, sl_, AF.Exp, [ksl], [ksl], scale=-1.0)
                tt(sl_, sl_, psum[pg][:, 0:w], ALU.mult, [ksl, PK[pg]], [ksl])
                st_[ti_] = (w, nch, ch0, tix, par, hbq, sc, sl_, ksl, t0)

            def S2(ti_):
                (w, nch, ch0, tix, par, hbq, sc, sl_, ksl, t0) = st_[ti_]
                po = psrot(0, 4)
                for q in range(nch):
                    n = ch0 + q
                    cs = slice(t0 + q * 128, t0 + (q + 1) * 128)
                    o_ = psum[po][:, q * 128:(q + 1) * 128]
                    mm(o_, VT[:, n, :], sc[0][:, q * 128:(q + 1) * 128], True, False, [("VT", n), ("hbs", 0, par)], [PK[po]])
                    mm(o_, SE[0][:, n, :], QT[0][:, cs], False, False, [("SE", 0, n), ("QT", 0, tix)], [PK[po]])
                    mm(o_, VT[:, n, :], sc[1][:, q * 128:(q + 1) * 128], False, False, [("VT", n), ("hbs", 1, par)], [PK[po]])
                    mm(o_, SE[1][:, n, :], QT[1][:, cs], False, True, [("SE", 1, n), ("QT", 1, tix)], [PK[po]])
                koc = ("ocL", par)
                oc = aux[:, par, 0:w]
                act(oc, psum[po][:, 0:w], AF.Copy, [PK[po]], [koc])
                sq = hbq[:, 2, 0:w]
                tt(sq, oc, oc, ALU.mult, [koc], [("hbq", par)])
                st_[ti_] = st_[ti_] + (oc, koc, sq)

            def S3(ti_):
                (w, nch, ch0, tix, par, hbq, sc, sl_, ksl, t0, oc, koc, sq) = st_[ti_]
                pss = psrot(6, 1)
                mm(psum[pss][:, 0:w], ones[:], sq, True, True, [("hbq", par), "c0"], [PK[pss]])
                rs = rsb[:, 0:w]
                rstd_from(rs, psum[pss][:, 0:w], [PK[pss]], "rsb", 128)
                tt(oc, oc, rs, ALU.mult, [koc, "rsb"], [koc])
                ogs = hbq[:, 3, 0:w]
                stt(ogs, oc, hng[:, hd:hd + 1], sl_, ALU.mult, ALU.mult, [koc, ksl, "c0"], [("hbo", par)])
                dma("sp", ogd[hd, :, t0:t0 + w], ogs, ("ogst", par), [("hbo", par)], [("ogd", hd, t0)])

            nT = len(T3)
            gnext = start_head(hd + 1) if hd + 1 < 8 else None
            R.tag = ('P2out', hd)
            for step in range(nT + 2):
                if 0 <= step - 2 < nT:
                    S3(step - 2)
                if 0 <= step - 1 < nT:
                    S2(step - 1)
                if step < nT:
                    S1(step)
                if gnext is not None and step >= 1:
                    R.tag = ('P2prep', hd + 1)
                    next(gnext, None)
                    R.tag = ('P2out', hd)
            if gnext is not None:
                R.tag = ('P2prep', hd + 1)
                for _ in gnext:
                    pass
        R.tag = None
        R.fence()
        _nslot[0] = NSLOT
        for kc in range(8):
            dma("sp", x[:, kc, :], xspill[:, kc * NT:(kc + 1) * NT], "reload", ["xspill"], xk(kc, 0, NT))
        so = [colblock(hg_w_out, 512 * q, 512) for q in range(2)]
        for (c0, c1, which, t0) in T3:
            w = c1 - c0
            for hd in range(8):
                dma("sp", actb[:, hd, 0:w], ogd[hd, :, t0:t0 + w], ("ogld", hd), [("ogd", hd, t0)], [("actb", hd)])
            for m in range(8):
                py = psrot(4, 2)
                s_, v = so[m // 4]
                for j in range(8):
                    mm(psum[py][:, 0:w], v[:, j, (m % 4) * 128:(m % 4 + 1) * 128], actb[:, j, 0:w], j == 0, j == 7,
                       [("slot", s_), ("actb", j)], [PK[py]])
                resid_add(i, 0, which, m, psum[py][:, 0:w], PK[py], c0, c1)
        dma("sp", cc2_in.ap()[:, 0:8], x[:, :, OWN0:OWN0 + 1].rearrange("p a b -> p (a b)"), "cc2st", xka(OWN0, OWN0 + 1), ["cc2in0"], slow=True)
        dma("sp", cc2_in.ap()[:, 8:16], x[:, :, OWN1 - 1:OWN1].rearrange("p a b -> p (a b)"), "cc2st", xka(OWN1 - 1, OWN1), ["cc2in1"], slow=True)
        R.add("pool", lambda e: e.collective_compute("AllGather", ALU.bypass, replica_groups=[[0, 1], [2, 3], [4, 5], [6, 7]],
                                                    ins=[cc2_in.ap().opt()], outs=[cc2_out.ap().opt()]), ["cc2in0", "cc2in1"], ["cc2out"], dma="cc2", inc=1, cost=1.0, lat=40.0)
        dma("sp", x[:, :, OWN0 - 1:OWN0].rearrange("p a b -> p (a b)"), cc2_out.ap()[0:128, 8:16], "cc2ld", ["cc2out"], xka(OWN0 - 1, OWN0), slow=True)
        dma("sp", x[:, :, OWN1:OWN1 + 1].rearrange("p a b -> p (a b)"), cc2_out.ap()[128:256, 0:8], "cc2ld", ["cc2out"], xka(OWN1, OWN1 + 1), slow=True)

    conv_tiles = seg_tiles(500)
    for i in range(min(nl, 3)):
        adaln(i)
        if "m" in SKIP:
            pass
        elif i == 0:
            mixer_conv(i, seg_tiles(500))
        elif i == 1:
            mixer_pool(i, seg_tiles(480))
        elif i == 2:
            sgu_tiles = [(12 + 512 * t, min(12 + 512 * (t + 1), NS - 12), 0, 12, NS - 12) for t in range(5)] + \
                        [(P0, P0 + 256, 1, P0, P0 + 256), (P1, P1 + 256, 1, P1, P1 + 256)]
            mixer_sgu(i, sgu_tiles)
        if "f" not in SKIP:
            ffn(i, conv_tiles if i < 2 else split(12, NS - 12, 500, 0, 12, NS - 12) + conv_tiles[-2:])
    if nl >= 4:
        adaln(3)
        mixer_hgrn(3)
        ffn(3, split(OWN0, OWN1, 500, 0, OWN0 - 1, OWN1 + 1) + conv_tiles[-2:])

    def final_out():
        outs = [(OWN0 + 512 * t, OWN0 + 512 * (t + 1), ysT, 512 * t) for t in range(4)] + [(P0, NT, ypT, 0)]
        for (c0, c1, dst, d0) in outs:
            n = c1 - c0
            hq, hqk = next_hb()
            for kc in range(8):
                act(hq[:, kc, 0:n], x[:, kc, c0:c1], AF.Square, xk(kc, c0, c1), [hqk])
            pb = psrot(6, 1)
            for kc in range(8):
                mm(psum[pb][:, 0:n], ones[:], hq[:, kc, 0:n], kc == 0, kc == 7, [hqk, "c0"], [PK[pb]])
            k1 = "rsb"
            rs = rsb[:, 0:n]
            rstd_from(rs, psum[pb][:, 0:n], [PK[pb]], k1, D)
            for kc in range(8):
                i2, k2 = tmp()
                y = tmpf[:, i2, 0:n]
                stt(y, x[:, kc, c0:c1], fgt[:, kc:kc + 1], rs, ALU.mult, ALU.mult, xk(kc, c0, c1) + [k1, "c0"], [k2])
                dma("sp", dst[kc * 128:(kc + 1) * 128, d0:d0 + n], y, ("st", i2), [k2], [])

    final_out()

    if os.environ.get("MK_NOSCHED", "") == "":
        est = schedule(R)
        print("[sched] estimated us:", round(est))
    pos = {}
    for e in ENGS:
        for i_, op in enumerate(R.ops[e]):
            pos[id(op)] = i_
    for e in ENGS:
        for op in R.ops[e]:
            best = {}
            for d in op.deps:
                if d.dsem is not None or d.fn is None:
                    continue
                b = best.get(d.eng)
                if b is None or pos[id(d)] > pos[id(b)]:
                    best[d.eng] = d
            op.deps = set(best.values())
            for d in op.deps:
                if not (d.eng == "pe" and e == "pe"):
                    d.sig = True
    sems = {}

    def getsem(name):
        if name not in sems:
            sems[name] = es.enter_context(nc.semaphore("s%d" % len(sems)))
        return sems[name]

    RS = 30000
    for e in ENGS:
        cnt = 0
        for op in R.ops[e]:
            if op.sig:
                op.ssem, op.sval = (e, cnt // RS), cnt % RS + 1
                cnt += 1
    if os.environ.get('MK_DEBUG'):
        print('SIGCOUNT', {e: (sum(1 for o in R.ops[e] if o.sig), len(R.ops[e])) for e in ENGS})
    dma_sems = set()
    for e in ENGS:
        for op in R.ops[e]:
            if op.dsem is not None:
                dma_sems.add(op.dsem)
    for e in ENGS:
        getsem((e, 0))
    for d_ in sorted(dma_sems, key=str):
        getsem(("dma", d_))
    for e in ENGS:
        for op in R.ops[e]:
            if op.sig:
                getsem(op.ssem)

    if os.environ.get('MK_DEBUG'):
        for k_, v_ in sems.items():
            print('SEM', v_, k_, R.dcnt.get(k_[1]) if k_[0] == 'dma' else '')
    block = es.enter_context(nc.Block())
    handles = {"pe": "tensor", "act": "scalar", "dve": "vector", "pool": "gpsimd", "sp": "sync"}

    def emit(e, eng):
        seen = {}
        for op in R.ops[e]:
            need = {}
            for d in op.deps:
                if d.dsem is not None:
                    continue
                elif d.eng == "pe" and e == "pe":
                    continue
                else:
                    k, v = d.ssem, d.sval
                if need.get(k, 0) < v:
                    need[k] = v
            for k_, v_ in op.dmaw.items():
                need[("dma", k_)] = v_
            for k, v in need.items():
                if seen.get(k, 0) < v:
                    eng.wait_ge(sems[k], v)
                    seen[k] = v
            if op.fn is None:
                continue
            ins = op.fn(eng)
            if op.dsem is not None:
                ins.then_inc(sems[("dma", op.dsem)], op.dinc)
            elif op.sig:
                ins.then_inc(sems[op.ssem], 1)
        if e == "sp":
            for d_ in dma_sems:
                eng.wait_ge(sems[("dma", d_)], R.dcnt[d_])

    @block.tensor
    def _(eng):
        emit("pe", eng)

    @block.scalar
    def _(eng):
        emit("act", eng)

    @block.vector
    def _(eng):
        emit("dve", eng)

    @block.gpsimd
    def _(eng):
        emit("pool", eng)

    @block.sync
    def _(eng):
        emit("sp", eng)

    es.close()
    return nc


def fm(v):
    v = np.asarray(v, np.float32)
    lead = v.shape[:-1]
    n = v.shape[-1] // 128
    v = v.reshape(lead + (n, 128))
    v = np.moveaxis(v, -1, 0)
    return np.ascontiguousarray(v.reshape(128, -1))


def kernel(x_prompt, x_sample, state_rec, c, c_ctx, ada_w, ada_b, norm_g, final_g,
           conv_w_in, conv_w_dw, conv_w_out, pool_w, pool_scale,
           sgu_w_in, sgu_norm_g, sgu_w_s, sgu_b_s, sgu_w_out,
           hgrn_w_in, hgrn_lb, hgrn_norm_g, hgrn_w_out,
           ffn_w_up, ffn_w_dw, ffn_w_down):
    f32 = lambda a: np.ascontiguousarray(np.asarray(a, np.float32))
    shared = {
        "ada_w": f32(ada_w), "ada_b": fm(ada_b), "norm_g": fm(norm_g), "final_g": fm(final_g),
        "conv_w_in": f32(conv_w_in[0]), "conv_dw": fm(conv_w_dw[0]), "conv_w_out": f32(conv_w_out[0]),
        "pool_w": f32(pool_w[0]), "pool_scale": fm(pool_scale[0]),
        "sgu_w_in": f32(sgu_w_in[0]), "sgu_ng": fm(sgu_norm_g[0]),
        "sgu_wsT": f32(np.transpose(np.asarray(sgu_w_s[0]), (0, 2, 1))),
        "sgu_bs": f32(np.asarray(sgu_b_s[0]).reshape(1, 1024)), "sgu_w_out": f32(sgu_w_out[0]),
        "hg_w_in": f32(hgrn_w_in[0]), "hg_lb": fm(hgrn_lb), "hg_ng": fm(hgrn_norm_g[0]), "hg_w_out": f32(hgrn_w_out[0]),
        "ffn_up": f32(ffn_w_up), "ffn_dw": fm(ffn_w_dw), "ffn_down": f32(ffn_w_down),
    }
    xs = np.asarray(x_sample, np.float32)
    xp = np.asarray(x_prompt, np.float32)
    st = np.asarray(state_rec, np.float32)
    in_maps = []
    for core in range(8):
        s, half = core // 2, core % 2
        t0 = OWN * half - H
        seg = np.zeros((NS, D), np.float32)
        lo, hi = max(t0, 0), min(t0 + NS, 4096)
        seg[lo - t0:hi - t0] = xs[s, lo:hi]
        xpc = xp[2 * core:2 * core + 2].reshape(512, D)
        cc = np.stack([np.asarray(c[s], np.float32), np.asarray(c_ctx, np.float32)], -1)
        cf = np.ascontiguousarray(cc.reshape(8, 128, 2).transpose(1, 0, 2).reshape(128, 16))
        meta = np.zeros((128, 4), np.float32)
        meta[:, 0] = 32 * half - 3
        meta[:, 1] = 1.0 if half == 1 else 0.0
        meta[:, 2] = 1.0 if half == 0 else 0.0
        s0 = np.zeros((2, 8, 128, 128), np.float32)
        s0[half] = st[s, 0, half]
        m = dict(shared)
        m.update({"xsT": np.ascontiguousarray(seg.T), "xpT": np.ascontiguousarray(xpc.T), "cfm": cf, "meta": meta, "s0": s0})
        in_maps.append(m)
    nc = build()
    res = run_bass_kernel_spmd(nc, in_maps[:NCORES], core_ids=list(range(NCORES)))
    y_prompt = np.zeros((16, 256, D), np.float32)
    y_sample = np.zeros((4, 4096, D), np.float32)
    new_state = np.zeros((16, 1, 2, 8, 128, 128), np.float32)
    for core in range(NCORES):
        r = res.results[core]
        s, half = core // 2, core % 2
        y_sample[s, OWN * half:OWN * (half + 1)] = np.asarray(r["ysT"]).T
        y_prompt[2 * core:2 * core + 2] = np.asarray(r["ypT"]).T.reshape(2, 256, D)
        new_state[2 * core:2 * core + 2, 0] = np.asarray(r["nst"])
    return y_prompt, y_sample, new_state
```

```python
import os
from contextlib import ExitStack
import numpy as np
import concourse.bass as bass
import concourse.mybir as mybir
from concourse.bass_utils import run_bass_kernel_spmd

F32, BF16 = mybir.dt.float32, mybir.dt.bfloat16
AF = mybir.ActivationFunctionType
ALU = mybir.AluOpType

D = 1024
H = 140
OWN = 2048
NS = OWN + 2 * H
OWN0, OWN1 = H, H + OWN
P0, P1 = NS, NS + 256
NT = NS + 512
DFF = 2816
NSLOT = 7
SLOT = 4096
EPS = 1e-6
ENGS = ["pe", "act", "dve", "pool", "sp"]
NL = int(os.environ.get("MK_LAYERS", "4"))
NCORES = int(os.environ.get("MK_CORES", "8"))
SKIP = os.environ.get("MK_SKIP", "")


class Op:
    __slots__ = ("eng", "fn", "deps", "sig", "dsem", "dval", "ssem", "sval", "dinc", "dmaw", "cost", "lat", "gidx", "epoch", "isfence", "preds", "npend", "ready", "succ", "done", "fin", "grp", "tag")


class Rec:
    def __init__(self):
        self.ops = {e: [] for e in ENGS}
        self.lastw = {}
        self.readers = {}
        self.dcnt = {}
        self.gcnt = 0
        self.epoch = 0
        self.tag = None

    def add(self, eng, fn, r=(), w=(), dma=None, inc=16, cost=None, lat=0.0):
        op = Op()
        op.cost = cost if cost is not None else (0.1 if eng in ('pool', 'sp') else 0.5)
        op.lat, op.gidx, op.epoch, op.isfence = lat, self.gcnt, self.epoch, False
        op.grp = None
        op.tag = self.tag
        self.gcnt += 1
        op.eng, op.fn, op.deps, op.sig, op.dsem, op.dval = eng, fn, set(), False, dma, 0
        op.dinc = inc
        op.dmaw = {}
        for k in r:
            x = self.lastw.get(k)
            if x is not None:
                op.deps.add(x)
        for k in w:
            x = self.lastw.get(k)
            if x is not None:
                op.deps.add(x)
            for rd in self.readers.get(k, ()):
                op.deps.add(rd)
        for k in r:
            self.readers.setdefault(k, []).append(op)
        for k in w:
            self.lastw[k] = op
            self.readers[k] = []
        op.deps.discard(op)
        for d_ in op.deps:
            if d_.dsem is not None:
                op.dmaw[d_.dsem] = self.dcnt[d_.dsem]
        if dma is not None:
            self.dcnt[dma] = self.dcnt.get(dma, 0) + inc
            op.dval = self.dcnt[dma]
        self.ops[eng].append(op)
        return op

    def fence(self):
        self.epoch += 1
        outs = []
        for e in ENGS:
            op = self.add(e, None, cost=0.0)
            op.isfence = True
            for k_, v_ in self.dcnt.items():
                op.dmaw[k_] = v_
            outs.append(op)
        return outs


def schedule(R, W=40, SYNC=0.35):
    import collections
    allops = sorted((op for e in ENGS for op in R.ops[e]), key=lambda o: o.gidx)
    dma_by_sem = collections.defaultdict(list)
    for op in allops:
        if op.dsem is not None:
            dma_by_sem[op.dsem].append(op)
    for op in allops:
        preds = set(op.deps)
        for sem, val in op.dmaw.items():
            for d in dma_by_sem[sem]:
                if d.dval <= val:
                    if d is not op:
                        preds.add(d)
                else:
                    break
        op.preds, op.succ, op.done, op.ready, op.fin = preds, [], False, 0.0, 0.0
    for op in allops:
        for p in op.preds:
            p.succ.append(op)
    new = {e: [] for e in ENGS}
    tnow = 0.0
    for ep in range(R.epoch + 1):
        seg = {e: [op for op in R.ops[e] if op.epoch == ep and not op.isfence] for e in ENGS}
        fences = {e: [op for op in R.ops[e] if op.epoch == ep and op.isfence] for e in ENGS}
        lasts = [new[e][-1] for e in ENGS if new[e]]
        for e in ENGS:
            for f in fences[e]:
                f.deps = set(l for l in lasts if l.fn is not None)
                f.done, f.fin = True, tnow
                new[e].append(f)
        teng = {e: tnow for e in ENGS}
        for e in ENGS:
            for op in seg[e]:
                op.npend = sum(1 for p in op.preds if not p.done)
                op.ready = max([tnow] + [p.fin + (0.0 if (p.eng == "pe" and e == "pe") else SYNC) for p in op.preds if p.done])
        head = {e: 0 for e in ENGS}
        left = sum(len(v) for v in seg.values())
        groups = {}
        for op in seg["pe"]:
            if op.grp is not None:
                groups.setdefault(op.grp, []).append(op)
        WPE = 1

        def commit(op, st):
            e = op.eng
            op.done = True
            teng[e] = st + op.cost
            op.fin = st + op.cost + op.lat
            new[e].append(op)
            for s_ in op.succ:
                if not s_.done and s_.epoch == ep:
                    s_.npend -= 1
                    r_ = op.fin + (0.0 if (op.eng == "pe" and s_.eng == "pe") else SYNC)
                    if r_ > s_.ready:
                        s_.ready = r_

        while left:
            best = None
            for e in ENGS:
                lst = seg[e]
                h = head[e]
                while h < len(lst) and lst[h].done:
                    h += 1
                head[e] = h
                if h >= len(lst):
                    continue
                cand, cnt, i = None, 0, h
                te = teng[e]
                if e == "pe":
                    seen_g = set()
                    while i < len(lst) and cnt < WPE:
                        op = lst[i]
                        i += 1
                        if op.done:
                            continue
                        g = op.grp
                        if g is not None:
                            if g in seen_g:
                                continue
                            seen_g.add(g)
                            mem = groups[g]
                        else:
                            mem = [op]
                        cnt += 1
                        ok, rd = True, tnow
                        for m in mem:
                            for p in m.preds:
                                if p.grp is not None and p.grp == g:
                                    continue
                                if not p.done:
                                    ok = False
                                    break
                                r_ = p.fin + (0.0 if p.eng == "pe" else SYNC)
                                if r_ > rd:
                                    rd = r_
                            if not ok:
                                break
                        if not ok:
                            continue
                        if rd <= te:
                            cand = (rd, mem)
                            break
                        if cand is None or rd < cand[0]:
                            cand = (rd, mem)
                    if cand is None:
                        continue
                    st = max(te, cand[0])
                    if best is None or st < best[0]:
                        best = (st, cand[1])
                    continue
                while i < len(lst) and cnt < W:
                    op = lst[i]
                    if not op.done:
                        cnt += 1
                        if op.npend == 0:
                            if op.ready <= te:
                                cand = op
                                break
                            if cand is None or op.ready < cand.ready:
                                cand = op
                    i += 1
                if cand is None:
                    continue
                st = max(te, cand.ready)
                if best is None or st < best[0]:
                    best = (st, [cand])
            assert best is not None, "scheduler deadlock"
            st, mem = best
            for m in mem:
                commit(m, max(st, teng[m.eng]))
                left -= 1
        tnow = max([tnow] + [op.fin for e in ENGS for op in seg[e]])
        if os.environ.get('MK_DEBUG'):
            print('EPOCH', ep, round(tnow), {e: round(sum(o.cost for o in seg[e])) for e in ENGS})
    if os.environ.get('MK_DEBUG'):
        import collections as _c
        tg = _c.OrderedDict()
        for e in ENGS:
            for op in new[e]:
                if op.tag is not None:
                    a_, b_ = tg.get(op.tag, (1e18, 0))
                    tg[op.tag] = (min(a_, op.fin - op.cost - op.lat), max(b_, op.fin))
        for k_, (a_, b_) in sorted(tg.items(), key=lambda kv: kv[1][0]):
            print('TAG', k_, round(a_), round(b_))
    for e in ENGS:
        R.ops[e] = new[e]
    return tnow


def build(nl=NL):
    nc = bass.Bass("TRN2", target_bir_lowering=False)
    R = Rec()
    es = ExitStack()

    def din(name, shape, dt=F32):
        return nc.dram_tensor(name, list(shape), dt, kind="ExternalInput").ap()

    def dout(name, shape):
        return nc.dram_tensor(name, list(shape), F32, kind="ExternalOutput").ap()

    xsT = din("xsT", [D, NS]); xpT = din("xpT", [D, 512])
    cfm = din("cfm", [128, 16]); meta = din("meta", [128, 4])
    s0 = din("s0", [2, 8, 128, 128])
    ada_w = din("ada_w", [4, D, 6 * D]); ada_b = din("ada_b", [128, 4 * 48])
    norm_g = din("norm_g", [128, 64]); final_g = din("final_g", [128, 8])
    conv_w_in = din("conv_w_in", [D, 3 * D]); conv_dw = din("conv_dw", [128, 24]); conv_w_out = din("conv_w_out", [D, D])
    pool_w = din("pool_w", [4, 256, 256]); pool_scale = din("pool_scale", [128, 8])
    sgu_w_in = din("sgu_w_in", [D, 2 * D]); sgu_ng = din("sgu_ng", [128, 8]); sgu_wsT = din("sgu_wsT", [8, 128, 128])
    sgu_bs = din("sgu_bs", [1, 1024]); sgu_w_out = din("sgu_w_out", [D, D])
    hg_w_in = din("hg_w_in", [D, 5 * D]); hg_lb = din("hg_lb", [128, 64]); hg_ng = din("hg_ng", [128, 8]); hg_w_out = din("hg_w_out", [D, D])
    ffn_up = din("ffn_up", [4, D, 2 * DFF]); ffn_dw = din("ffn_dw", [128, 4 * 3 * 44]); ffn_down = din("ffn_down", [4, DFF, D])
    ysT = dout("ysT", [D, OWN]); ypT = dout("ypT", [D, 512]); nst = dout("nst", [2, 2, 8, 128, 128])
    xspill = nc.dram_tensor("xspill", [128, 8 * NT], F32).ap()
    hcache = nc.dram_tensor("hcache", [8, 128, 8, 512], BF16).ap()
    cc1_in = nc.dram_tensor("cc1_in", [2048, 128], F32); cc1_out = nc.dram_tensor("cc1_out", [4096, 128], F32)
    cc2_in = nc.dram_tensor("cc2_in", [128, 16], F32); cc2_out = nc.dram_tensor("cc2_out", [256, 16], F32)

    def sb(name, shape, dt=F32):
        return es.enter_context(nc.sbuf_tensor(name, list(shape), dt))

    def ps(name, shape, dt=F32):
        return es.enter_context(nc.psum_tensor(name, list(shape), dt))

    x = sb("x", [128, 8, NT])
    ring = sb("ring", [128, NSLOT, SLOT], BF16)
    hb = sb("hb", [128, 8, 512], BF16)
    actb = sb("actb", [128, 8, 512], BF16)
    NTMP = 6
    tmpf = sb("tmpf", [128, NTMP, 512])
    hb2 = sb("hb2", [128, 8, 512], BF16)
    rmask = sb("rmask", [128, 512])
    sbs = sb("sbs", [128, 8, 128])
    rsb = sb("rsb", [128, 512])
    aux = sb("aux", [128, 4, 512])
    vtm = aux[:].bitcast(BF16)
    _c32 = sb("c32", [128, 2420])
    _c16 = sb("c16", [128, 832], BF16)
    _off = {"32": 0, "16": 0}

    def carve(shape, which="32"):
        n = int(np.prod(shape[1:]))
        base = _c32 if which == "32" else _c16
        v = base[:, _off[which]:_off[which] + n]
        _off[which] += n
        if len(shape) == 3:
            v = v.rearrange("p (a b) -> p a b", a=shape[1])
        elif len(shape) == 4:
            v = v.rearrange("p (a b c) -> p a b c", a=shape[1], b=shape[2])
        elif len(shape) == 5:
            v = v.rearrange("p (a b c d) -> p a b c d", a=shape[1], b=shape[2], c=shape[3])
        return v

    maskt = carve([128, 2 * H], "16")
    ones = carve([128, 128], "16")
    ident = carve([128, 128], "16")
    mtri = carve([128, 2, 128], "16")
    csil = carve([128, 8, 2], "16")
    iot = carve([128, 128])
    iop = carve([128, 1])
    craw = carve([128, 8, 2])
    metat = carve([128, 4])
    modt = carve([128, 4, 48, 2])
    adab = carve([128, 4, 48])
    acoef = carve([128, 4, 2, 8, 2])
    ngt = carve([128, 4, 2, 8])
    fgt = carve([128, 8])
    cdw = carve([128, 3, 8])
    pscl = carve([128, 8]); pgs = carve([128, 8, 2])
    sng = carve([128, 8])
    lbt = carve([128, 4, 2, 8]); lbw = carve([128, 6, 2, 8])
    hng = carve([128, 8])
    fdw = carve([128, 4, 3, 44])
    pet = carve([128, 2, 1, 64])
    petr = carve([128, 4, 40]); petc = carve([128, 4, 64])
    small = carve([128, 64])
    epsb = carve([128, 1])
    one1 = carve([128, 1])
    xhb = carve([128, 2, 8, 8])
    rw = carve([128, 4])

    psum = [ps(f"ps{i}", [128, 512]) for i in range(7)]
    psb = ps("psb", [128, 1024], BF16)
    PK = [("ps", i) for i in range(7)]

    def xk(kc, a, b):
        return [("x", kc, blk) for blk in range(a // 128, (b - 1) // 128 + 1)]

    def xka(a, b):
        out = []
        for kc in range(8):
            out += xk(kc, a, b)
        return out

    _grp = [0]

    def fsz(ap):
        return int(np.prod(ap.shape[1:]))

    def mm(out, lhsT, rhs, start, stop, r, w):
        if start:
            _grp[0] += 1
        op = R.add("pe", lambda e: e.matmul(out, lhsT, rhs, start=start, stop=stop), r, w, cost=max(fsz(out), 64) / 1900.0 + 0.03)
        op.grp = _grp[0]
        return op

    def act(out, in_, func, r, w, scale=1.0, bias=None, accum=None):
        def fn(e):
            kw = {}
            if bias is not None:
                kw["bias"] = bias
            if accum is not None:
                kw["accum_out"] = accum
            return e.activation(out=out, in_=in_, func=func, scale=scale, **kw)
        return R.add("act", fn, r, w, cost=0.25 + fsz(out) / 1400.0)

    def tt(out, in0, in1, op, r, w, eng="dve"):
        return R.add(eng, lambda e: e.tensor_tensor(out=out, in0=in0, in1=in1, op=op), r, w, cost=(0.1 + fsz(out) / 960.0) * (1.5 if eng == 'pool' else 1.0))

    def stt(out, in0, scalar, in1, op0, op1, r, w):
        return R.add("dve", lambda e: e.scalar_tensor_tensor(out=out, in0=in0, scalar=scalar, in1=in1, op0=op0, op1=op1), r, w, cost=0.1 + fsz(out) / 960.0)

    def ts(out, in0, s1, s2, op0, op1, r, w, eng="dve"):
        if s2 is None:
            return R.add(eng, lambda e: e.tensor_scalar(out=out, in0=in0, scalar1=s1, scalar2=None, op0=op0), r, w, cost=0.1 + fsz(out) / 960.0)
        return R.add(eng, lambda e: e.tensor_scalar(out=out, in0=in0, scalar1=s1, scalar2=s2, op0=op0, op1=op1), r, w, cost=0.1 + fsz(out) / 960.0)

    def cp(out, in_, r, w, eng="dve"):
        return R.add(eng, lambda e: e.tensor_copy(out=out, in_=in_), r, w, cost=0.1 + fsz(out) / 960.0)

    def mset(ap, val, w, eng="dve"):
        return R.add(eng, lambda e: e.memset(ap, val), (), w, cost=0.08 + fsz(ap) / 1900.0)

    def rstd_from(rs, src, r, key, dim):
        act(rs, src, AF.Ln, r + ["c0"], [key], scale=1.0 / dim, bias=epsb[:, 0:1])
        act(rs, rs, AF.Exp, [key], [key], scale=-0.5)

    _hb = [0]

    def next_hb():
        _hb[0] += 1
        return (hb, "hb0") if _hb[0] % 2 else (hb2, "hb1")

    def recip(out, in_, r, w):
        return R.add("dve", lambda e: e.reciprocal(out=out, in_=in_), r, w, cost=0.1 + 4 * fsz(out) / 960.0)

    def dma(eng, out, in_, sem, r, w, slow=False):
        if slow:
            return R.add(eng, lambda e: e.dma_start(out=out, in_=in_, allow_slow_non_contiguous=True), r, w, dma=sem, cost=0.15, lat=4.0)
        return R.add(eng, lambda e: e.dma_start(out=out, in_=in_), r, w, dma=sem, cost=0.15, lat=2.0 + fsz(out) * 128 * 4 / 200e3)

    _tmp = [0]

    def tmp():
        i = _tmp[0] % NTMP
        _tmp[0] += 1
        return i, ("tmp", i)

    _slot = [0]
    _nslot = [NSLOT]

    def wload(src_ap, nelem, shape_str=None, **kw):
        s = _slot[0] % _nslot[0]
        _slot[0] += 1
        dst = ring[:, s, 0:nelem]
        if shape_str:
            dst = dst.rearrange(shape_str, **kw)
        dma("pool", dst, src_ap, ("slot", s), (), [("slot", s)])
        return s, dst

    def colblock(W, c0, w):
        return wload(W[:, c0:c0 + w].rearrange("(kc p) c -> p kc c", p=128), 8 * w, "p (kc c) -> p kc c", kc=8)

    _psr = [0]

    def psrot(lo, n):
        i = lo + _psr[0] % n
        _psr[0] += 1
        return i

    _ls = [0]

    def load_small(dst, src, last=False):
        _ls[0] += 1
        dma("sp", dst, src, "ld0", (), ["c0"] if last else [("c0x", _ls[0])])

    load_small(craw[:].rearrange("p a b -> p (a b)"), cfm)
    load_small(metat[:], meta)
    load_small(adab[:].rearrange("p a b -> p (a b)"), ada_b)
    load_small(ngt[:].rearrange("p a b c -> p (a b c)"), norm_g)
    load_small(cdw[:].rearrange("p a b -> p (a b)"), conv_dw)
    load_small(pscl[:], pool_scale)
    load_small(sng[:], sgu_ng)
    load_small(sbs[:].rearrange("p a b -> p (a b)"), sgu_bs.partition_broadcast(128))
    load_small(lbt[:].rearrange("p a b c -> p (a b c)"), hg_lb)
    load_small(hng[:], hg_ng)
    load_small(fdw[:].rearrange("p a b c -> p (a b c)"), ffn_dw)
    for kc in range(8):
        dma("sp", x[:, kc, 0:NS], xsT[kc * 128:(kc + 1) * 128, :], "ld0", (), xk(kc, 0, NS))
        dma("sp", x[:, kc, P0:NT], xpT[kc * 128:(kc + 1) * 128, :], "ld0", (), xk(kc, P0, NT))
    load_small(fgt[:], final_g, last=True)

    C0 = ["c0"]
    mset(ones[:], 1.0, C0)
    mset(epsb[:], EPS, C0)
    mset(one1[:], 1.0, C0)
    R.add("pool", lambda e: e.iota(iot[:], [[1, 128]], base=0, channel_multiplier=0, allow_small_or_imprecise_dtypes=True), (), ["c0i"])
    R.add("pool", lambda e: e.iota(iop[:], [[0, 1]], base=0, channel_multiplier=1, allow_small_or_imprecise_dtypes=True), (), ["c0i"])
    CI = ["c0", "c0i"]
    ts(ident[:], iot[:], iop[:, 0:1], None, ALU.is_equal, None, CI, ["cid"])
    ts(mtri[:, 0, :], iot[:], iop[:, 0:1], None, ALU.is_ge, None, CI, ["cid"])
    ts(mtri[:, 1, :], iot[:], iop[:, 0:1], None, ALU.is_le, None, CI, ["cid"])
    mset(rmask[:], 1.0, ["cid"])
    for q in range(4):
        mset(rmask[:, q * 128:q * 128 + 1], 0.0, ["cid"])
    mset(maskt[:], 1.0, ["cid"])
    ts(maskt[:, 0:H], maskt[:, 0:H], metat[:, 1:2], None, ALU.mult, None, ["c0", "cid"], ["cid"])
    ts(maskt[:, H:2 * H], maskt[:, H:2 * H], metat[:, 2:3], None, ALU.mult, None, ["c0", "cid"], ["cid"])
    C = ["c0", "c0i", "cid"]
    act(csil[:].rearrange("p a b -> p (a b)"), craw[:].rearrange("p a b -> p (a b)"), AF.Silu, C, ["csil"])
    LBE = small[:, 0:64].rearrange("p (a b) -> p a b", a=4)
    act(small[:, 0:64], lbt[:].rearrange("p a b c -> p (a b c)"), AF.Exp, C, ["lb0"])
    lbv = lbw[:].rearrange("p a b c -> p a (b c)")
    tt(lbv[:, 3], LBE[:, 1], LBE[:, 2], ALU.add, ["lb0"], ["lb1"])
    tt(lbv[:, 3], lbv[:, 3], LBE[:, 3], ALU.add, ["lb1"], ["lb1"])
    tt(lbv[:, 4], lbv[:, 3], LBE[:, 0], ALU.add, ["lb0", "lb1"], ["lb2"])
    recip(lbv[:, 4], lbv[:, 4], ["lb2"], ["lb2"])
    tt(lbv[:, 0], lbv[:, 3], lbv[:, 4], ALU.mult, ["lb1", "lb2"], ["lb3"])
    ts(lbv[:, 1], lbv[:, 0], -1.0, 1.0, ALU.mult, ALU.add, ["lb3"], ["lb4"])
    ts(lbv[:, 2], lbv[:, 1], -1.0, None, ALU.mult, None, ["lb4"], ["lb5"])
    LB = ["lb3", "lb4", "lb5"]

    act(pet[:, 0, 0, 0:1], iop[:, 0:1], AF.Exp, CI, ["pe0"], scale=-float(np.log(10000.0)) / 256.0)
    ts(pet[:, 0, 0, 1:2], pet[:, 0, 0, 0:1], float(np.exp(-np.log(10000.0) / 2)), None, ALU.mult, None, ["pe0"], ["pe0"])
    jr = pet[:, 1, 0, 0:40]
    ts(jr, iot[:, 0:40], metat[:, 0:1], None, ALU.add, None, C, ["pe1"])

    def sincos_table(dst, idx_ap, n, m, cosine, key):
        i1, k1 = tmp(); i2, k2 = tmp()
        y = tmpf[:, i1, 0:n]; r_ = tmpf[:, i2, 0:n]
        ts(y, idx_ap, pet[:, 0, 0, m:m + 1], float(1.0 / (2 * np.pi)), ALU.mult, ALU.mult, ["pe0", "pe1"] + C, [k1])
        if cosine:
            ts(y, y, 0.25, None, ALU.add, None, [k1], [k1])
        yi = tmpf[:, i2, 0:n].bitcast(mybir.dt.int32)
        cp(yi, y, [k1], [k2])
        cp(r_, yi, [k2], [k2])
        tt(y, y, r_, ALU.subtract, [k1, k2], [k1])
        ts(r_, y, 0.5, None, ALU.is_gt, None, [k1], [k2])
        tt(y, y, r_, ALU.subtract, [k1, k2], [k1])
        ts(r_, y, -0.5, None, ALU.is_lt, None, [k1], [k2])
        tt(y, y, r_, ALU.add, [k1, k2], [k1])
        act(dst, y, AF.Sin, [k1], [key], scale=6.28318)

    for m in range(2):
        sincos_table(petr[:, m, 0:40], jr, 40, m, False, "petr")
        sincos_table(petr[:, 2 + m, 0:40], jr, 40, m, True, "petr")
        sincos_table(petc[:, m, :], iot[:, 0:64], 64, m, False, "petc")
        sincos_table(petc[:, 2 + m, :], iot[:, 0:64], 64, m, True, "petc")
    NB = (NS - 12) // 64
    for kc in range(4):
        r_, w_ = ["petr"] + xk(kc, 0, NS), xk(kc, 0, NS)
        tt(x[:, kc, 0:12], x[:, kc, 0:12], petr[:, kc, 0:1].to_broadcast([128, 12]), ALU.add, r_, w_)
        for b0 in range(0, NB, 8):
            nb = min(8, NB - b0)
            xa = x[:, kc, 12 + 64 * b0:12 + 64 * (b0 + nb)].rearrange("p (a b) -> p a b", b=64)
            tt(xa, xa, petr[:, kc, 1 + b0:1 + b0 + nb].unsqueeze(2).to_broadcast([128, nb, 64]), ALU.add, r_, w_)
        tt(x[:, kc, NS - 12:NS], x[:, kc, NS - 12:NS], petr[:, kc, 37:38].to_broadcast([128, 12]), ALU.add, r_, w_)
    for kc in range(4, 8):
        r_, w_ = ["petc"] + xk(kc, 0, NS), xk(kc, 0, NS)
        tt(x[:, kc, 0:12], x[:, kc, 0:12], petc[:, kc - 4, 52:64], ALU.add, r_, w_)
        for b0 in range(0, NB, 8):
            nb = min(8, NB - b0)
            xa = x[:, kc, 12 + 64 * b0:12 + 64 * (b0 + nb)].rearrange("p (a b) -> p a b", b=64)
            tt(xa, xa, petc[:, kc - 4, :].unsqueeze(1).to_broadcast([128, nb, 64]), ALU.add, r_, w_)
        tt(x[:, kc, NS - 12:NS], x[:, kc, NS - 12:NS], petc[:, kc - 4, 0:12], ALU.add, r_, w_)

    def adaln(i):
        pb = psrot(6, 1)
        pso = psum[pb][:, 0:96].rearrange("p (a b) -> p a b", b=2)
        for q in range(12):
            s, wv = colblock(ada_w[i], q * 512, 512)
            for nn in range(4):
                n = q * 4 + nn
                for kc in range(8):
                    mm(pso[:, n, :], wv[:, kc, nn * 128:(nn + 1) * 128], csil[:, kc, :], kc == 0, kc == 7,
                       [("slot", s), "csil"], [PK[pb]])
        tt(modt[:, i], pso, adab[:, i, :].unsqueeze(2).to_broadcast([128, 48, 2]), ALU.add, [PK[pb]] + C, [("mod", i)])
        for site in range(2):
            stt(acoef[:, i, site], modt[:, i, 8 + 24 * site:16 + 24 * site, :], 1.0,
                ngt[:, i, site, :].unsqueeze(2).to_broadcast([128, 8, 2]), ALU.add, ALU.mult, [("mod", i)] + C, [("mod", i)])

    def coefs(i, site, which):
        a = lambda kc: acoef[:, i, site, kc, which:which + 1]
        b = lambda kc: modt[:, i, 24 * site + kc, which:which + 1]
        g = lambda kc: modt[:, i, 24 * site + 16 + kc, which:which + 1]
        return a, b, g

    def split(lo, hi, maxw, which, slo, shi):
        n = -(-(hi - lo) // maxw)
        base, rem = divmod(hi - lo, n)
        out, c = [], lo
        for t in range(n):
            w_ = base + (1 if t < rem else 0)
            out.append((c, c + w_, which, slo, shi))
            c += w_
        return out

    def seg_tiles(maxw, slo=0, shi=NS):
        return split(slo, shi, maxw, 0, slo, shi) + [(P0, P0 + 256, 1, P0, P0 + 256), (P1, P1 + 256, 1, P1, P1 + 256)]

    def make_h(i, site, c0, c1, halo, which, slo, shi, mask=True, dst=None, dkey=None, doff=None):
        a0, a1 = max(c0 - halo, slo), min(c1 + halo, shi)
        n, off = a1 - a0, a0 - (c0 - halo)
        ntot = c1 - c0 + 2 * halo
        a, b, g = coefs(i, site, which)
        if dst is None:
            dst, dkey = next_hb()
        if doff is not None:
            off = doff
        DK = dkey if isinstance(dkey, list) else [dkey]
        MK = [("mod", i)]
        for kc in range(8):
            act(dst[:, kc, off:off + n], x[:, kc, a0:a1], AF.Square, xk(kc, a0, a1), DK)
        pb = psrot(6, 1)
        for kc in range(8):
            mm(psum[pb][:, 0:n], ones[:], dst[:, kc, off:off + n], kc == 0, kc == 7, DK + ["c0"], [PK[pb]])
        k1 = "rsb"
        rs = rsb[:, 0:n]
        rstd_from(rs, psum[pb][:, 0:n], [PK[pb]], k1, D)
        for kc in range(8):
            i2, k2 = tmp()
            xr = tmpf[:, i2, 0:n]
            tt(xr, x[:, kc, a0:a1], rs, ALU.mult, xk(kc, a0, a1) + [k1], [k2])
            act(dst[:, kc, off:off + n], xr, AF.Identity, [k2] + MK, DK, scale=a(kc), bias=b(kc))
        if mask and which == 0:
            for (m0, m1, mo) in ((0, H, 0), (OWN1, NS, H)):
                lo_, hi_ = max(a0, m0), min(a1, m1)
                if lo_ < hi_:
                    v = dst[:, :, off + lo_ - a0:off + hi_ - a0]
                    tt(v, v, maskt[:, mo + lo_ - m0:mo + hi_ - m0].unsqueeze(1).to_broadcast([128, 8, hi_ - lo_]),
                       ALU.mult, DK + ["cid"], DK)
        if doff is None and off > 0:
            mset(dst[:, :, 0:off], 0.0, DK)
        if doff is None and off + n < ntot:
            mset(dst[:, :, off + n:ntot], 0.0, DK)
        return ntot, dst, dkey

    def dwconv_from_psum(pt, pk, w, wcol, ykey_i):
        iy, ky = ykey_i
        y = tmpf[:, iy, 0:w]
        act(y, pt[:, 1:1 + w], AF.Identity, [pk, "c0"], [ky], scale=wcol(1))
        stt(y, pt[:, 0:w], wcol(0), y, ALU.mult, ALU.add, [pk, ky, "c0"], [ky])
        stt(y, pt[:, 2:2 + w], wcol(2), y, ALU.mult, ALU.add, [pk, ky, "c0"], [ky])
        return y

    def resid_add(i, site, which, m, pt, pk, c0, c1, extra_scale=None, defer=None):
        a, b, g = coefs(i, site, which)
        sc = g(m) if extra_scale is None else extra_scale(m)
        rk = [pk, ("mod", i), "pgs"]
        if defer is None:
            stt(x[:, m, c0:c1], pt, sc, x[:, m, c0:c1], ALU.mult, ALU.add, rk + xk(m, c0, c1), xk(m, c0, c1))
        else:
            d_, sl_ = defer
            w_ = c1 - c0
            stt(x[:, m, c0:c1 - d_], pt[:, 0:w_ - d_], sc, x[:, m, c0:c1 - d_], ALU.mult, ALU.add, rk + xk(m, c0, c1 - d_), xk(m, c0, c1 - d_))
            stt(xhb[:, sl_, m, 0:d_], pt[:, w_ - d_:w_], sc, x[:, m, c1 - d_:c1], ALU.mult, ALU.add, rk + xk(m, c1 - d_, c1), [("xh", sl_, m)])

    def flush(pend):
        if pend is None:
            return
        c1, d_, sl_ = pend
        cp(x[:, :, c1 - d_:c1], xhb[:, sl_, :, 0:d_], [("xh", sl_, m) for m in range(8)], xka(c1 - d_, c1))

    def adjacent(tiles, idx):
        return idx + 1 < len(tiles) and tiles[idx + 1][0] == tiles[idx][1] and tiles[idx + 1][3:5] == tiles[idx][3:5]

    def ffn(i, tiles):
        for grp in range(3):
            pieces = []
            for q in (2 * grp, 2 * grp + 1):
                wq = 512 if q < 5 else 256
                sa, va = colblock(ffn_up[i], 512 * q, wq)
                sb_, vb = colblock(ffn_up[i], DFF + 512 * q, wq)
                sd, vd = wload(ffn_down[i][512 * q:512 * q + wq, :].rearrange("(jj p) n -> p jj n", p=128),
                               (wq // 128) * 1024, "p (jj n) -> p jj n", n=1024)
                pieces.append((wq // 128, sa, va, sb_, vb, sd, vd))
            chunks = []
            for pi_, (nj, sa, va, sb_, vb, sd, vd) in enumerate(pieces):
                for jj in range(nj):
                    chunks.append((sa, va, sb_, vb, sd, vd, jj, 4 * (2 * grp + pi_) + jj))
            nck = len(chunks)

            def setup_h(ti):
                (c0, c1, which, slo, shi) = tiles[ti]
                n = c1 - c0 + 2
                if grp == 0:
                    _, hq, hqk = make_h(i, 1, c0, c1, 1, which, slo, shi)
                    dma("sp", hcache[ti, :, :, 0:n], hq[:, :, 0:n], ("hcst", hqk), [hqk], [("hc", ti)])
                else:
                    hq, hqk = next_hb()
                    dma("sp", hq[:, :, 0:n], hcache[ti, :, :, 0:n], ("hcld", hqk), [("hc", ti)], [hqk])
                return hq, hqk, n

            def chunk_mm(hs, ci):
                hq, hqk, n = hs
                (sa, va, sb_, vb, sd, vd, jj, j) = chunks[ci]
                pa = psrot(0, 4); pbk = psrot(0, 4)
                for kc in range(8):
                    mm(psum[pa][:, 0:n], va[:, kc, jj * 128:(jj + 1) * 128], hq[:, kc, 0:n], kc == 0, kc == 7,
                       [("slot", sa), hqk], [PK[pa]])
                for kc in range(8):
                    mm(psum[pbk][:, 0:n], vb[:, kc, jj * 128:(jj + 1) * 128], hq[:, kc, 0:n], kc == 0, kc == 7,
                       [("slot", sb_), hqk], [PK[pbk]])
                return pa, pbk

            def chunk_ew(ci, w, pa, pbk):
                j = chunks[ci][7]
                ya = dwconv_from_psum(psum[pa], PK[pa], w, lambda t, j=j: fdw[:, i, t, j:j + 1], tmp())
                kya = ("tmp", (_tmp[0] - 1) % NTMP)
                yb = dwconv_from_psum(psum[pbk], PK[pbk], w, lambda t, j=j: fdw[:, i, t, 22 + j:23 + j], tmp())
                kyb = ("tmp", (_tmp[0] - 1) % NTMP)
                act(ya, ya, AF.Silu, [kya], [kya])
                tt(actb[:, ci, 0:w], ya, yb, ALU.mult, [kya, kyb], [("actb", ci)])

            hs = setup_h(0)
            pre = {}
            for ti, (c0, c1, which, slo, shi) in enumerate(tiles):
                w = c1 - c0
                hs_next = None
                for ci in range(nck):
                    banks = pre.pop(ci) if ci in pre else chunk_mm(hs, ci)
                    chunk_ew(ci, w, *banks)
                    if ci == min(3, nck - 1) and ti + 1 < len(tiles):
                        hs_next = setup_h(ti + 1)
                if hs_next is not None:
                    for ci in range(2):
                        pre[ci] = chunk_mm(hs_next, ci)
                for m in range(8):
                    py = psrot(4, 2)
                    for ci in range(nck):
                        (sa, va, sb_, vb, sd, vd, jj, j) = chunks[ci]
                        mm(psum[py][:, 0:w], vd[:, jj, m * 128:(m + 1) * 128], actb[:, ci, 0:w], ci == 0, ci == nck - 1,
                           [("slot", sd), ("actb", ci)], [PK[py]])
                    resid_add(i, 1, which, m, psum[py][:, 0:w], PK[py], c0, c1)
                hs = hs_next

    def mixer_conv(i, tiles):
        for grp in range(2):
            sl = [colblock(conv_w_in, 1024 * bi + 512 * grp, 512) for bi in range(3)]
            so_, vo = wload(conv_w_out[512 * grp:512 * grp + 512, :].rearrange("(jj p) n -> p jj n", p=128), 4096, "p (jj n) -> p jj n", n=1024)

            def setup_h(ti):
                (c0, c1, which, slo, shi) = tiles[ti]
                n = c1 - c0 + 4
                if grp == 0:
                    _, hq, hqk = make_h(i, 0, c0, c1, 2, which, slo, shi)
                    dma("sp", hcache[ti, :, :, 0:n], hq[:, :, 0:n], ("hcst", hqk), [hqk], [("hc", ti)])
                else:
                    hq, hqk = next_hb()
                    dma("sp", hq[:, :, 0:n], hcache[ti, :, :, 0:n], ("hcld", hqk), [("hc", ti)], [hqk])
                return hq, hqk, n

            def chunk_mm(hs, jj):
                hq, hqk, n = hs
                pp = [psrot(0, 4) for _ in range(3)]
                for bi in range(3):
                    s_, v = sl[bi]
                    for kc in range(8):
                        mm(psum[pp[bi]][:, 0:n], v[:, kc, jj * 128:(jj + 1) * 128], hq[:, kc, 0:n], kc == 0, kc == 7,
                           [("slot", s_), hqk], [PK[pp[bi]]])
                return pp

            def chunk_ew(jj, w, pp):
                j = 4 * grp + jj
                nm = w + 2
                ic, kcg = tmp(); im, km = tmp()
                cgs = tmpf[:, ic, 0:nm]; mt = tmpf[:, im, 0:nm]
                act(cgs, psum[pp[1]][:, 1:1 + nm], AF.Copy, [PK[pp[1]]], [kcg])
                tt(mt, cgs, psum[pp[2]][:, 1:1 + nm], ALU.mult, [kcg, PK[pp[2]]], [km])
                iy, ky = tmp()
                y = tmpf[:, iy, 0:w]
                act(y, mt[:, 1:1 + w], AF.Identity, [km, "c0"], [ky], scale=cdw[:, 1, j:j + 1])
                stt(y, mt[:, 0:w], cdw[:, 0, j:j + 1], y, ALU.mult, ALU.add, [km, ky, "c0"], [ky])
                stt(y, mt[:, 2:2 + w], cdw[:, 2, j:j + 1], y, ALU.mult, ALU.add, [km, ky, "c0"], [ky])
                tt(actb[:, jj, 0:w], y, psum[pp[0]][:, 2:2 + w], ALU.mult, [ky, PK[pp[0]]], [("actb", jj)])

            hs = setup_h(0)
            pre = {}
            for ti, (c0, c1, which, slo, shi) in enumerate(tiles):
                w = c1 - c0
                hs_next = None
                for jj in range(4):
                    pp = pre.pop(jj) if jj in pre else chunk_mm(hs, jj)
                    chunk_ew(jj, w, pp)
                    if jj == 1 and ti + 1 < len(tiles):
                        hs_next = setup_h(ti + 1)
                if hs_next is not None:
                    pre[0] = chunk_mm(hs_next, 0)
                for m in range(8):
                    py = psrot(4, 2)
                    for jj in range(4):
                        mm(psum[py][:, 0:w], vo[:, jj, m * 128:(m + 1) * 128], actb[:, jj, 0:w], jj == 0, jj == 3,
                           [("slot", so_), ("actb", jj)], [PK[py]])
                    resid_add(i, 0, which, m, psum[py][:, 0:w], PK[py], c0, c1)
                hs = hs_next

    def mixer_pool(i, tiles):
        s, wv = wload(pool_w.rearrange("g (kc p) n -> p g kc n", p=128), 2048, "p (g kc n) -> p g kc n", g=4, kc=2)
        tt(pgs[:], modt[:, i, 16:24, :], pscl[:].unsqueeze(2).to_broadcast([128, 8, 2]), ALU.mult, [("mod", i)] + C, ["pgs"])
        for g_ in range(4):
            mset(rw[:, g_:g_ + 1], 1.0 / (2 << g_), ["rw"])
        HL = 8
        pend = None

        def levels(src, ksrc, n, grp):
            i_a, k_a = tmp()
            s_ = tmpf[:, i_a, 0:n]
            tt(s_[:, 1:n], src[:, 0:n - 1], src[:, 1:n], ALU.add, [ksrc], [k_a])
            d = 1
            for lev in range(grp):
                i_b, k_b = tmp()
                s2 = tmpf[:, i_b, 0:n]
                tt(s2[:, d:n - d], s_[:, 0:n - 2 * d], s_[:, 2 * d:n], ALU.add, [k_a], [k_b])
                s_, k_a, d = s2, k_b, d * 2
            return s_, k_a

        for ti, (c0, c1, which, slo, shi) in enumerate(tiles):
            dfr = (HL, ti % 2) if adjacent(tiles, ti) else None
            w = c1 - c0
            n = w + 2 * HL
            a0, a1 = max(c0 - HL, slo), min(c1 + HL, shi)
            off, nv = a0 - (c0 - HL), a1 - a0
            interior = (which == 0 and a0 >= H and a1 <= OWN1 and nv == n)
            kmk = "aux0"
            mk = aux[:, 0, 0:n]
            if not interior:
                mset(mk, 0.0, [kmk])
                mset(mk[:, off:off + nv], 1.0, [kmk])
                if which == 0:
                    for (m0, m1, mo) in ((0, H, 0), (OWN1, NS, H)):
                        lo_, hi_ = max(a0, m0), min(a1, m1)
                        if lo_ < hi_:
                            cp(mk[:, off + lo_ - a0:off + hi_ - a0], maskt[:, mo + lo_ - m0:mo + hi_ - m0], [kmk, "cid"], [kmk])
            hq, hqk = next_hb()
            pb = psrot(6, 1)
            for kc in range(8):
                act(hq[:, kc, 0:nv], x[:, kc, a0:a1], AF.Square, xk(kc, a0, a1), [hqk])
            for kc in range(8):
                mm(psum[pb][:, 0:nv], ones[:], hq[:, kc, 0:nv], kc == 0, kc == 7, [hqk, "c0"], [PK[pb]])
            krs = "rsb"
            rs = rsb[:, 0:nv]
            rstd_from(rs, psum[pb][:, 0:nv], [PK[pb]], krs, D)
            a, b, g = coefs(i, 0, which)
            for grp in range(4):
                kcn = "aux2"
                cn = aux[:, 2, 0:n]
                if not interior:
                    s_, k_a = levels(mk, kmk, n, grp)
                    ts(cn, s_, 1.0, None, ALU.max, None, [k_a], [kcn])
                    recip(cn, cn, [kcn], [kcn])
                for kk in range(2):
                    kc = 2 * grp + kk
                    kh = ("aux3", kk)
                    hfp = aux[:, 1 if kk else 3, 0:n]
                    if not interior:
                        mset(hfp, 0.0, [kh])
                    ix, kx = tmp()
                    xr = tmpf[:, ix, 0:nv]
                    tt(xr, x[:, kc, a0:a1], rs, ALU.mult, xk(kc, a0, a1) + [krs], [kx])
                    act(hfp[:, off:off + nv], xr, AF.Identity, [kx, ("mod", i)], [kh], scale=a(kc), bias=b(kc))
                    if not interior:
                        tt(hfp, hfp, mk, ALU.mult, [kh, kmk], [kh])
                    s_, k_a = levels(hfp, kh, n, grp)
                    if interior:
                        stt(actb[:, kc, 0:w], s_[:, HL:HL + w], rw[:, grp:grp + 1], hfp[:, HL:HL + w], ALU.mult, ALU.subtract,
                            [k_a, kh, "rw"], [("actb", kc)])
                    else:
                        tt(s_, s_, cn, ALU.mult, [k_a, kcn], [k_a])
                        tt(actb[:, kc, 0:w], s_[:, HL:HL + w], hfp[:, HL:HL + w], ALU.subtract, [k_a, kh], [("actb", kc)])
                for nn in range(2):
                    py = psrot(4, 2)
                    for kk in range(2):
                        mm(psum[py][:, 0:w], wv[:, grp, kk, nn * 128:(nn + 1) * 128], actb[:, 2 * grp + kk, 0:w], kk == 0, kk == 1,
                           [("slot", s), ("actb", 2 * grp + kk)], [PK[py]])
                    m = 2 * grp + nn
                    resid_add(i, 0, which, m, psum[py][:, 0:w], PK[py], c0, c1, extra_scale=lambda m_, wh=which: pgs[:, m_, wh:wh + 1], defer=dfr)
            flush(pend)
            pend = (c1, HL, ti % 2) if dfr else None

    def mixer_sgu(i, tiles):
        sl = [colblock(sgu_w_in, 512 * q, 512) for q in range(4)]
        so = [colblock(sgu_w_out, 512 * q, 512) for q in range(2)]
        sw, wsv = wload(sgu_wsT.rearrange("g q p -> q g p"), 1024, "q (g p) -> q g p", g=8)
        def sgu_h(ti):
            (c0_, c1_, wh_, slo_, shi_) = tiles[ti]
            _, hq_, hqk_ = make_h(i, 0, c0_, c1_, 0, wh_, slo_, shi_, mask=False)
            return hq_, hqk_

        hnext = sgu_h(0)
        for ti, (c0, c1, which, slo, shi) in enumerate(tiles):
            w = c1 - c0
            nch = w // 128
            hq, hqk = hnext
            for j in range(8):
                pu = psrot(0, 4)
                s, v = sl[j // 4]
                for kc in range(8):
                    mm(psum[pu][:, 0:w], v[:, kc, (j % 4) * 128:(j % 4 + 1) * 128], hq[:, kc, 0:w], kc == 0, kc == 7,
                       [("slot", s), hqk], [PK[pu]])
                act(actb[:, j, 0:w], psum[pu][:, 0:w], AF.Gelu_apprx_tanh, [PK[pu]], [("actb", j)])
                if j == 3 and ti + 1 < len(tiles):
                    hnext = sgu_h(ti + 1)
            for q in range(nch):
                iv0, kv0 = tmp(); iv1, kv1 = tmp()
                vts = [tmpf[:, iv0, :], tmpf[:, iv1, :]]
                kvs = [kv0, kv1]
                iss, kss = tmp()
                ss = tmpf[:, iss, 0:4]
                for half in range(2):
                    pv = psrot(0, 4)
                    s, v = sl[2 + half]
                    for kc in range(8):
                        mm(psum[pv][:, :], hq[:, kc, q * 128:(q + 1) * 128], v[:, kc, :], kc == 0, kc == 7,
                           [("slot", s), hqk], [PK[pv]])
                    act(vts[half], psum[pv][:, :], AF.Gelu_apprx_tanh, [PK[pv]], [kvs[half]])
                    ij, kj = tmp()
                    act(tmpf[:, ij, :], vts[half], AF.Square, [kvs[half]], [kj, kss], accum=ss[:, half:half + 1])
                tt(ss[:, 2:3], ss[:, 0:1], ss[:, 1:2], ALU.add, [kss], [kss])
                rstd_from(ss[:, 3:4], ss[:, 2:3], [kss], kss, D)
                for half in range(2):
                    ts(vtm[:, q, half * 512:(half + 1) * 512], vts[half], ss[:, 3:4], None, ALU.mult, None, [kvs[half], kss], [("vtm", q)])
            for gi in range(8):
                pg = psrot(0, 4)
                for q in range(nch):
                    mm(psum[pg][:, q * 128:(q + 1) * 128], vtm[:, q, gi * 128:(gi + 1) * 128], wsv[:, gi, :], True, True,
                       [("slot", sw), ("vtm", q)], [PK[pg]])
                isg, ksg = tmp()
                sg_ = tmpf[:, isg, 0:w]
                stt(sg_.rearrange("p (a b) -> p a b", b=128), psum[pg][:, 0:w].rearrange("p (a b) -> p a b", b=128), sng[:, gi:gi + 1],
                    sbs[:, gi, :].unsqueeze(1).to_broadcast([128, nch, 128]), ALU.mult, ALU.add, [PK[pg]] + C, [ksg])
                tt(actb[:, gi, 0:w], actb[:, gi, 0:w], sg_, ALU.mult, [ksg, ("actb", gi)], [("actb", gi)])
            for m in range(8):
                py = psrot(4, 2)
                s, v = so[m // 4]
                for j in range(8):
                    mm(psum[py][:, 0:w], v[:, j, (m % 4) * 128:(m % 4 + 1) * 128], actb[:, j, 0:w], j == 0, j == 7,
                       [("slot", s), ("actb", j)], [PK[py]])
                resid_add(i, 0, which, m, psum[py][:, 0:w], PK[py], c0, c1)


    def mixer_hgrn(i):
        actf = actb[:].rearrange("p a b -> p (a b)").bitcast(F32).rearrange("p (a b) -> p a b", a=4)
        LT = [(tmpf[:, j_, :], ("tmp", j_)) for j_ in range(NTMP)] + [(aux[:, j_, :], ("auxL", j_)) for j_ in range(2, 4)] + \
             [(actf[:, j_, :], ("actL", j_)) for j_ in range(3)]
        _lt = [0]

        def tmpL():
            v = LT[_lt[0] % len(LT)]
            _lt[0] += 1
            return v

        xflat = x[:].rearrange("p a b -> p (a b)")
        xb16 = xflat.bitcast(BF16)
        QT = [xb16[:, 2560 * d:2560 * (d + 1)] for d in range(2)]
        KT = [xb16[:, 5120 + 2560 * d:5120 + 2560 * (d + 1)] for d in range(2)]
        KH = [xb16[:, 10240 + 2560 * d:10240 + 2560 * (d + 1)].rearrange("p (n c) -> p n c", c=128) for d in range(2)]
        VT = xb16[:, 15360:17920].rearrange("p (n c) -> p n c", c=128)
        SE = [xb16[:, 17920 + 2560 * d:17920 + 2560 * (d + 1)].rearrange("p (n c) -> p n c", c=128) for d in range(2)]
        S32p = [xflat[:, 21760:22016].rearrange("p (d c) -> p d c", d=2), actf[:, 3, 0:256].rearrange("p (d c) -> p d c", d=2)]
        Dt = xflat[:, 22016:22056].rearrange("p (d c) -> p d c", d=2)
        s0t = xflat[:, 22060:22316].rearrange("p (d c) -> p d c", d=2)
        Gt = xflat[:, 22320:22576].rearrange("p (d c) -> p d c", d=2)
        h3r = ring[:, 2:7, :].rearrange("p s e -> p (s e)").rearrange("p (kc c) -> p kc c", kc=8)
        h3 = xb16[:, 23040:43520].rearrange("p (kc c) -> p kc c", kc=8)
        HKR = [("slot", s_) for s_ in range(2, 7)]
        HK = ["h3"]
        T3 = [(OWN0 + 512 * t, OWN0 + 512 * (t + 1), 0, 512 * t) for t in range(4)] + [(P0, P0 + 256, 1, 2048), (P1, P1 + 256, 1, 2304)]
        SEQ = [(0, 16, "S"), (16, 2, 0), (18, 2, 1)]
        mset(h3r[:, 0, 0:1], 0.0, HKR + [("h3t", t0_) for (_a, _b, _c, t0_) in T3])
        for (c0, c1, which, t0) in T3:
            make_h(i, 0, c0, c1, 0, which, c0, c1, mask=False, dst=h3r, dkey=[("h3t", t0)], doff=t0)
        for kc in range(8):
            dma("sp", xspill[:, kc * NT:(kc + 1) * NT], x[:, kc, :], "spill", xk(kc, 0, NT), ["xspill"])
        R.fence()
        for kc in range(8):
            if kc % 2:
                act(h3[:, kc, :], h3r[:, kc, :], AF.Copy, HKR, HK)
            else:
                cp(h3[:, kc, :], h3r[:, kc, :], HKR, HK)

        TIDX = {}
        for ti_, (c0_, c1_, wh_, t0_) in enumerate(T3):
            for n_ in range(t0_ // 128, (t0_ + c1_ - c0_) // 128):
                TIDX[n_] = ti_
        hbufs = [hb, hb2]

        def prep_gen(hd, wv, sw, tiles, full):
            pendB = []

            def flushB(keep):
                while len(pendB) > keep:
                    pendB.pop(0)()

            for (c0, c1, which, t0) in tiles:
                w = c1 - c0
                nch = w // 128
                ch0 = t0 // 128
                tix = TIDX[ch0]
                par = tix % 2
                WK = [("slot", sw)] + HK
                pv = psrot(0, 4)
                for q in range(nch):
                    for kc in range(8):
                        mm(psum[pv][:, q * 128:(q + 1) * 128], h3[:, kc, t0 + q * 128:t0 + (q + 1) * 128], wv[:, kc, 3, :], kc == 0, kc == 7, WK, [PK[pv]])
                act(VT[:, ch0:ch0 + nch, :], psum[pv][:, 0:w].rearrange("p (n c) -> p n c", c=128), AF.Copy, [PK[pv]],
                    [("VT", n_) for n_ in range(ch0, ch0 + nch)])
                pq = None
                if full:
                    pq = psrot(4, 2)
                    for kc in range(8):
                        mm(psum[pq][:, 0:w], wv[:, kc, 0, :], h3[:, kc, t0:t0 + w], kc == 0, kc == 7, WK, [PK[pq]])
                for d in range(2):
                    pz = psrot(0, 4)
                    for kc in range(8):
                        mm(psum[pz][:, 0:w], wv[:, kc, 1 + d, :], h3[:, kc, t0:t0 + w], kc == 0, kc == 7, WK, [PK[pz]])
                    ia, ka = tmpL(); ib, kb = tmpL(); ic2, kc2 = tmpL(); ic, kc_ = tmpL()
                    A = ia[:, 0:w]; B = ib[:, 0:w]; C2 = ic2[:, 0:w]; Cc = ic[:, 0:w]
                    lb0 = lbw[:, 0, d, hd:hd + 1]; lb1 = lbw[:, 1, d, hd:hd + 1]
                    act(A, psum[pz][:, 0:w], AF.Exp, [PK[pz]], [ka], scale=-1.0)
                    act(B, A, AF.Ln, [ka, "c0"] + LB, [kb], scale=lb0, bias=one1[:, 0:1])
                    act(C2, A, AF.Ln, [ka, "c0"], [kc2], bias=one1[:, 0:1])
                    tt(B, B, C2, ALU.subtract, [kb, kc2], [kb])
                    act(C2, C2, AF.Exp, [kc2], [kc2], scale=-1.0)
                    stt(A, A, lb1, C2, ALU.mult, ALU.mult, [ka, kc2] + LB, [ka])
                    R.add("dve", lambda e, Cc=Cc, B=B, w=w: e.tensor_tensor_scan(out=Cc, data0=rmask[:, 0:w], data1=B, initial=0.0,
                                                                                op0=ALU.mult, op1=ALU.add), [kb, "cid"], [kc_], cost=0.1 + 2 * w / 960.0)
                    C3 = Cc.rearrange("p (n c) -> p n c", c=128)
                    if d == 0:
                        bc, kbc = Cc, kc_
                        last = C3[:, :, 127:128]
                    else:
                        tt(B, B, Cc, ALU.subtract, [kb, kc_], [kb])
                        B3 = B.rearrange("p (n c) -> p n c", c=128)
                        tt(B3, B3, C3[:, :, 127:128].to_broadcast([128, nch, 128]), ALU.add, [kb, kc_], [kb])
                        bc, kbc = B, kb
                        last = B3[:, :, 0:1]
                    act(Dt[:, d, ch0:ch0 + nch].unsqueeze(2), last, AF.Exp, [kbc], [("Dt", d, tix)])
                    if full:
                        E = C2
                        act(E, bc, AF.Exp, [kbc, kc2], [kc2])
                        tt(QT[d][:, t0:t0 + w], E, psum[pq][:, 0:w], ALU.mult, [kc2, PK[pq]], [("QT", d, tix)])
                    ien, ken = tmpL()
                    En = ien[:, 0:w]
                    act(En, bc, AF.Exp, [kbc], [ken], scale=-1.0)
                    tt(En, A, En, ALU.mult, [ka, ken], [ken])
                    if full:
                        act(KT[d][:, t0:t0 + w], En, AF.Copy, [ken], [("KT", d, tix)])
                    kh = hbufs[par][:, 4 + d, 0:w]
                    tt(kh.rearrange("p (n c) -> p n c", c=128), En.rearrange("p (n c) -> p n c", c=128),
                       Dt[:, d, ch0:ch0 + nch].unsqueeze(2).to_broadcast([128, nch, 128]), ALU.mult, [ken, ("Dt", d, tix)], [("hbk", d, par)])
                    def stageB(d=d, par=par, kh=kh, nch=nch, ch0=ch0, w=w):
                        for q in range(nch):
                            R.add("pe", lambda e, q=q, kh=kh, d=d: e.transpose(psb[:, d * 512 + q * 128:d * 512 + (q + 1) * 128], kh[:, q * 128:(q + 1) * 128], ident[:]),
                                  [("hbk", d, par), "cid"], [("psb", d)], cost=0.12)
                        act(KH[d][:, ch0:ch0 + nch, :], psb[:, d * 512:d * 512 + w].rearrange("p (n c) -> p n c", c=128), AF.Copy, [("psb", d)],
                            [("KH", d, n_) for n_ in range(ch0, ch0 + nch)])

                    pendB.append(stageB)
                    flushB(int(os.environ.get('MK_LAG', '2')))
                yield
            flushB(0)

        def prep(hd, wv, sw, tiles, full):
            for _ in prep_gen(hd, wv, sw, tiles, full):
                pass

        _pp = {0: 0, 1: 0}

        def s32cur(d):
            return S32p[_pp[d]][:, d, :], ("S32", d, _pp[d])

        def sweep(hd, d, c_first, nchk, store):
            order = range(c_first, c_first + nchk) if d == 0 else range(c_first + nchk - 1, c_first - 1, -1)
            for n in order:
                cur, kcur = s32cur(d)
                if store:
                    act(SE[d][:, n, :], cur, AF.Copy, [kcur], [("SE", d, n)])
                pu = psrot(0, 6)
                mm(psum[pu][:, 0:128], KH[d][:, n, :], VT[:, n, :], True, True, [("KH", d, n), ("VT", n)], [PK[pu]])
                _pp[d] ^= 1
                nxt, knxt = s32cur(d)
                stt(nxt, cur, Dt[:, d, n:n + 1], psum[pu][:, 0:128], ALU.mult, ALU.add,
                    [kcur, ("Dt", d, TIDX[n]), PK[pu]], [knxt])

        def head_piece(hd):
            sl_ = _slot[0] % _nslot[0]
            _slot[0] += 1
            view = ring[:, sl_, 0:4096].rearrange("p (kc s c) -> p kc s c", kc=8, s=4)
            for sblk in range(4):
                c0_ = sblk * 1024 + hd * 128
                dma("pool", view[:, :, sblk, :], hg_w_in[:, c0_:c0_ + 128].rearrange("(kc p) c -> p kc c", p=128),
                    ("slot", sl_), (), [("slot", sl_)])
            return sl_, view

        for hd in range(8):
            R.tag = ('P1prep', hd)
            sw, wv = head_piece(hd)
            dma("sp", s0t[:, 0, :], s0[0, hd], ("s0ld", 0), [], [("s0t", 0)])
            dma("sp", s0t[:, 1, :], s0[1, hd], ("s0ld", 1), [], [("s0t", 1)])
            prep(hd, wv, sw, T3[0:4], False)
            R.tag = ('P1sweep', hd)
            for d in range(2):
                cur, kcur = s32cur(d)
                cp(cur, s0t[:, d, :], [("s0t", d)], [kcur])
                sweep(hd, d, 0, 16, False)
                cur, kcur = s32cur(d)
                dma("sp", cc1_in.ap()[(d * 8 + hd) * 128:(d * 8 + hd + 1) * 128, :], cur, ("cc1st", d), [kcur], [("cc1in", hd, d)])
        allin = [("cc1in", hd, d) for hd in range(8) for d in range(2)]
        R.add("pool", lambda e: e.collective_compute("AllGather", ALU.bypass, replica_groups=[[0, 1], [2, 3], [4, 5], [6, 7]],
                                                    ins=[cc1_in.ap().opt()], outs=[cc1_out.ap().opt()]), allin, ["cc1out"], dma="cc1", inc=1, cost=1.0, lat=40.0)
        ogd = nc.dram_tensor("ogd", [8, 128, 2560], BF16).ap()
        HW = {}

        def start_head(hd):
            R.tag = ('P2prep', hd)
            sw, wv = head_piece(hd)
            sg_, wg = colblock(hg_w_in, 4096 + hd * 128, 128)
            HW[hd] = (sg_, wg)
            dma("sp", s0t[:, 0, :], s0[0, hd], ("s0ld", 0), [], [("s0t", 0)])
            dma("sp", s0t[:, 1, :], s0[1, hd], ("s0ld", 1), [], [("s0t", 1)])
            dma("sp", Gt[:, 0, :], cc1_out.ap()[hd * 128:(hd + 1) * 128, :], ("gld", 0), ["cc1out"], [("Gt", 0)])
            dma("sp", Gt[:, 1, :], cc1_out.ap()[2048 + (8 + hd) * 128:2048 + (9 + hd) * 128, :], ("gld", 1), ["cc1out"], [("Gt", 1)])
            return prep_gen(hd, wv, sw, T3, True)

        gen0 = start_head(0)
        for _ in gen0:
            pass
        for hd in range(8):
            sg_, wg = HW[hd]
            R.tag = ('P2sweep', hd)
            for (c_first, nchk, kind) in SEQ:
                for d in range(2):
                    cur, kcur = s32cur(d)
                    if kind == "S":
                        stt(cur, Gt[:, d, :], metat[:, 1 + d:2 + d], s0t[:, d, :], ALU.mult, ALU.add,
                            [("Gt", d), ("s0t", d), "c0"], [kcur])
                    else:
                        mset(cur, 0.0, [kcur])
                    sweep(hd, d, c_first, nchk, True)
                    if kind != "S":
                        cur, kcur = s32cur(d)
                        dma("sp", nst[kind, d, hd], cur, ("nst", d, _pp[d]), [kcur], [])
            R.tag = ('P2out', hd)
            st_ = {}

            def S1(ti_):
                (c0, c1, which, t0) = T3[ti_]
                w = c1 - c0
                nch = w // 128
                ch0 = t0 // 128
                tix = TIDX[ch0]
                par = tix % 2
                hbq = hbufs[par]
                sc = [hbq[:, d, 0:w] for d in range(2)]
                for d in range(2):
                    pscore = psrot(0, 4)
                    for q in range(nch):
                        cs = slice(t0 + q * 128, t0 + (q + 1) * 128)
                        mm(psum[pscore][:, q * 128:(q + 1) * 128], KT[d][:, cs], QT[d][:, cs], True, True, [("KT", d, tix), ("QT", d, tix)], [PK[pscore]])
                    tt(sc[d].rearrange("p (n c) -> p n c", c=128), psum[pscore][:, 0:w].rearrange("p (n c) -> p n c", c=128),
                       mtri[:, d, :].unsqueeze(1).to_broadcast([128, nch, 128]), ALU.mult, [PK[pscore], "cid"], [("hbs", d, par)])
                pg = psrot(0, 4)
                for kc in range(8):
                    mm(psum[pg][:, 0:w], wg[:, kc, :], h3[:, kc, t0:t0 + w], kc == 0, kc == 7, [("slot", sg_)] + HK, [PK[pg]])
                ksl = ("hsl", par)
                sl_ = hbq[:, 6:8, :].rearrange("p a b -> p (a b)").bitcast(F32)[:, 0:w]
                act(sl_, psum[pg][:, 0:w], AF.Exp, [PK[pg]], [ksl], scale=-1.0)
                act(sl_, sl_, AF.Ln, [ksl, "c0"], [ksl], bias=one1[:, 0:1])
                act(sl_, sl_, AF.Exp, [ksl], [ksl], scale=-1.0)
                tt(sl_, sl_, psum[pg][:, 0:w], ALU.mult, [ksl, PK[pg]], [ksl])
                st_[ti_] = (w, nch, ch0, tix, par, hbq, sc, sl_, ksl, t0)

            def S2(ti_):
                (w, nch, ch0, tix, par, hbq, sc, sl_, ksl, t0) = st_[ti_]
                po = psrot(0, 4)
                for q in range(nch):
                    n = ch0 + q
                    cs = slice(t0 + q * 128, t0 + (q + 1) * 128)
                    o_ = psum[po][:, q * 128:(q + 1) * 128]
                    mm(o_, VT[:, n, :], sc[0][:, q * 128:(q + 1) * 128], True, False, [("VT", n), ("hbs", 0, par)], [PK[po]])
                    mm(o_, SE[0][:, n, :], QT[0][:, cs], False, False, [("SE", 0, n), ("QT", 0, tix)], [PK[po]])
                    mm(o_, VT[:, n, :], sc[1][:, q * 128:(q + 1) * 128], False, False, [("VT", n), ("hbs", 1, par)], [PK[po]])
                    mm(o_, SE[1][:, n, :], QT[1][:, cs], False, True, [("SE", 1, n), ("QT", 1, tix)], [PK[po]])
                koc = ("ocL", par)
                oc = aux[:, par, 0:w]
                act(oc, psum[po][:, 0:w], AF.Copy, [PK[po]], [koc])
                sq = hbq[:, 2, 0:w]
                tt(sq, oc, oc, ALU.mult, [koc], [("hbq", par)])
                st_[ti_] = st_[ti_] + (oc, koc, sq)

            def S3(ti_):
                (w, nch, ch0, tix, par, hbq, sc, sl_, ksl, t0, oc, koc, sq) = st_[ti_]
                pss = psrot(6, 1)
                mm(psum[pss][:, 0:w], ones[:], sq, True, True, [("hbq", par), "c0"], [PK[pss]])
                rs = rsb[:, 0:w]
                rstd_from(rs, psum[pss][:, 0:w], [PK[pss]], "rsb", 128)
                tt(oc, oc, rs, ALU.mult, [koc, "rsb"], [koc])
                ogs = hbq[:, 3, 0:w]
                stt(ogs, oc, hng[:, hd:hd + 1], sl_, ALU.mult, ALU.mult, [koc, ksl, "c0"], [("hbo", par)])
                dma("sp", ogd[hd, :, t0:t0 + w], ogs, ("ogst", par), [("hbo", par)], [("ogd", hd, t0)])

            nT = len(T3)
            gnext = start_head(hd + 1) if hd + 1 < 8 else None
            R.tag = ('P2out', hd)
            for step in range(nT + 2):
                if 0 <= step - 2 < nT:
                    S3(step - 2)
                if 0 <= step - 1 < nT:
                    S2(step - 1)
                if step < nT:
                    S1(step)
                if gnext is not None and step >= 1:
                    R.tag = ('P2prep', hd + 1)
                    next(gnext, None)
                    R.tag = ('P2out', hd)
            if gnext is not None:
                R.tag = ('P2prep', hd + 1)
                for _ in gnext:
                    pass
        R.tag = None
        R.fence()
        _nslot[0] = NSLOT
        for kc in range(8):
            dma("sp", x[:, kc, :], xspill[:, kc * NT:(kc + 1) * NT], "reload", ["xspill"], xk(kc, 0, NT))
        so = [colblock(hg_w_out, 512 * q, 512) for q in range(2)]
        for (c0, c1, which, t0) in T3:
            w = c1 - c0
            for hd in range(8):
                dma("sp", actb[:, hd, 0:w], ogd[hd, :, t0:t0 + w], ("ogld", hd), [("ogd", hd, t0)], [("actb", hd)])
            for m in range(8):
                py = psrot(4, 2)
                s_, v = so[m // 4]
                for j in range(8):
                    mm(psum[py][:, 0:w], v[:, j, (m % 4) * 128:(m % 4 + 1) * 128], actb[:, j, 0:w], j == 0, j == 7,
                       [("slot", s_), ("actb", j)], [PK[py]])
                resid_add(i, 0, which, m, psum[py][:, 0:w], PK[py], c0, c1)
        dma("sp", cc2_in.ap()[:, 0:8], x[:, :, OWN0:OWN0 + 1].rearrange("p a b -> p (a b)"), "cc2st", xka(OWN0, OWN0 + 1), ["cc2in0"], slow=True)
        dma("sp", cc2_in.ap()[:, 8:16], x[:, :, OWN1 - 1:OWN1].rearrange("p a b -> p (a b)"), "cc2st", xka(OWN1 - 1, OWN1), ["cc2in1"], slow=True)
        R.add("pool", lambda e: e.collective_compute("AllGather", ALU.bypass, replica_groups=[[0, 1], [2, 3], [4, 5], [6, 7]],
                                                    ins=[cc2_in.ap().opt()], outs=[cc2_out.ap().opt()]), ["cc2in0", "cc2in1"], ["cc2out"], dma="cc2", inc=1, cost=1.0, lat=40.0)
        dma("sp", x[:, :, OWN0 - 1:OWN0].rearrange("p a b -> p (a b)"), cc2_out.ap()[0:128, 8:16], "cc2ld", ["cc2out"], xka(OWN0 - 1, OWN0), slow=True)
        dma("sp", x[:, :, OWN1:OWN1 + 1].rearrange("p a b -> p (a b)"), cc2_out.ap()[128:256, 0:8], "cc2ld", ["cc2out"], xka(OWN1, OWN1 + 1), slow=True)

    conv_tiles = seg_tiles(500)
    for i in range(min(nl, 3)):
        adaln(i)
        if "m" in SKIP:
            pass
        elif i == 0:
            mixer_conv(i, seg_tiles(500))
        elif i == 1:
            mixer_pool(i, seg_tiles(480))
        elif i == 2:
            sgu_tiles = [(12 + 512 * t, min(12 + 512 * (t + 1), NS - 12), 0, 12, NS - 12) for t in range(5)] + \
                        [(P0, P0 + 256, 1, P0, P0 + 256), (P1, P1 + 256, 1, P1, P1 + 256)]
            mixer_sgu(i, sgu_tiles)
        if "f" not in SKIP:
            ffn(i, conv_tiles if i < 2 else split(12, NS - 12, 500, 0, 12, NS - 12) + conv_tiles[-2:])
    if nl >= 4:
        adaln(3)
        mixer_hgrn(3)
        ffn(3, split(OWN0, OWN1, 500, 0, OWN0 - 1, OWN1 + 1) + conv_tiles[-2:])

    def final_out():
        outs = [(OWN0 + 512 * t, OWN0 + 512 * (t + 1), ysT, 512 * t) for t in range(4)] + [(P0, NT, ypT, 0)]
        for (c0, c1, dst, d0) in outs:
            n = c1 - c0
            hq, hqk = next_hb()
            for kc in range(8):
                act(hq[:, kc, 0:n], x[:, kc, c0:c1], AF.Square, xk(kc, c0, c1), [hqk])
            pb = psrot(6, 1)
            for kc in range(8):
                mm(psum[pb][:, 0:n], ones[:], hq[:, kc, 0:n], kc == 0, kc == 7, [hqk, "c0"], [PK[pb]])
            k1 = "rsb"
            rs = rsb[:, 0:n]
            rstd_from(rs, psum[pb][:, 0:n], [PK[pb]], k1, D)
            for kc in range(8):
                i2, k2 = tmp()
                y = tmpf[:, i2, 0:n]
                stt(y, x[:, kc, c0:c1], fgt[:, kc:kc + 1], rs, ALU.mult, ALU.mult, xk(kc, c0, c1) + [k1, "c0"], [k2])
                dma("sp", dst[kc * 128:(kc + 1) * 128, d0:d0 + n], y, ("st", i2), [k2], [])

    final_out()

    if os.environ.get("MK_NOSCHED", "") == "":
        est = schedule(R)
        print("[sched] estimated us:", round(est))
    pos = {}
    for e in ENGS:
        for i_, op in enumerate(R.ops[e]):
            pos[id(op)] = i_
    for e in ENGS:
        for op in R.ops[e]:
            best = {}
            for d in op.deps:
                if d.dsem is not None or d.fn is None:
                    continue
                b = best.get(d.eng)
                if b is None or pos[id(d)] > pos[id(b)]:
                    best[d.eng] = d
            op.deps = set(best.values())
            for d in op.deps:
                if not (d.eng == "pe" and e == "pe"):
                    d.sig = True
    sems = {}

    def getsem(name):
        if name not in sems:
            sems[name] = es.enter_context(nc.semaphore("s%d" % len(sems)))
        return sems[name]

    RS = 30000
    for e in ENGS:
        cnt = 0
        for op in R.ops[e]:
            if op.sig:
                op.ssem, op.sval = (e, cnt // RS), cnt % RS + 1
                cnt += 1
    if os.environ.get('MK_DEBUG'):
        print('SIGCOUNT', {e: (sum(1 for o in R.ops[e] if o.sig), len(R.ops[e])) for e in ENGS})
    dma_sems = set()
    for e in ENGS:
        for op in R.ops[e]:
            if op.dsem is not None:
                dma_sems.add(op.dsem)
    for e in ENGS:
        getsem((e, 0))
    for d_ in sorted(dma_sems, key=str):
        getsem(("dma", d_))
    for e in ENGS:
        for op in R.ops[e]:
            if op.sig:
                getsem(op.ssem)

    if os.environ.get('MK_DEBUG'):
        for k_, v_ in sems.items():
            print('SEM', v_, k_, R.dcnt.get(k_[1]) if k_[0] == 'dma' else '')
    block = es.enter_context(nc.Block())
    handles = {"pe": "tensor", "act": "scalar", "dve": "vector", "pool": "gpsimd", "sp": "sync"}

    def emit(e, eng):
        seen = {}
        for op in R.ops[e]:
            need = {}
            for d in op.deps:
                if d.dsem is not None:
                    continue
                elif d.eng == "pe" and e == "pe":
                    continue
                else:
                    k, v = d.ssem, d.sval
                if need.get(k, 0) < v:
                    need[k] = v
            for k_, v_ in op.dmaw.items():
                need[("dma", k_)] = v_
            for k, v in need.items():
                if seen.get(k, 0) < v:
                    eng.wait_ge(sems[k], v)
                    seen[k] = v
            if op.fn is None:
                continue
            ins = op.fn(eng)
            if op.dsem is not None:
                ins.then_inc(sems[("dma", op.dsem)], op.dinc)
            elif op.sig:
                ins.then_inc(sems[op.ssem], 1)
        if e == "sp":
            for d_ in dma_sems:
                eng.wait_ge(sems[("dma", d_)], R.dcnt[d_])

    @block.tensor
    def _(eng):
        emit("pe", eng)

    @block.scalar
    def _(eng):
        emit("act", eng)

    @block.vector
    def _(eng):
        emit("dve", eng)

    @block.gpsimd
    def _(eng):
        emit("pool", eng)

    @block.sync
    def _(eng):
        emit("sp", eng)

    es.close()
    return nc


def fm(v):
    v = np.asarray(v, np.float32)
    lead = v.shape[:-1]
    n = v.shape[-1] // 128
    v = v.reshape(lead + (n, 128))
    v = np.moveaxis(v, -1, 0)
    return np.ascontiguousarray(v.reshape(128, -1))


def kernel(x_prompt, x_sample, state_rec, c, c_ctx, ada_w, ada_b, norm_g, final_g,
           conv_w_in, conv_w_dw, conv_w_out, pool_w, pool_scale,
           sgu_w_in, sgu_norm_g, sgu_w_s, sgu_b_s, sgu_w_out,
           hgrn_w_in, hgrn_lb, hgrn_norm_g, hgrn_w_out,
           ffn_w_up, ffn_w_dw, ffn_w_down):
    f32 = lambda a: np.ascontiguousarray(np.asarray(a, np.float32))
    shared = {
        "ada_w": f32(ada_w), "ada_b": fm(ada_b), "norm_g": fm(norm_g), "final_g": fm(final_g),
        "conv_w_in": f32(conv_w_in[0]), "conv_dw": fm(conv_w_dw[0]), "conv_w_out": f32(conv_w_out[0]),
        "pool_w": f32(pool_w[0]), "pool_scale": fm(pool_scale[0]),
        "sgu_w_in": f32(sgu_w_in[0]), "sgu_ng": fm(sgu_norm_g[0]),
        "sgu_wsT": f32(np.transpose(np.asarray(sgu_w_s[0]), (0, 2, 1))),
        "sgu_bs": f32(np.asarray(sgu_b_s[0]).reshape(1, 1024)), "sgu_w_out": f32(sgu_w_out[0]),
        "hg_w_in": f32(hgrn_w_in[0]), "hg_lb": fm(hgrn_lb), "hg_ng": fm(hgrn_norm_g[0]), "hg_w_out": f32(hgrn_w_out[0]),
        "ffn_up": f32(ffn_w_up), "ffn_dw": fm(ffn_w_dw), "ffn_down": f32(ffn_w_down),
    }
    xs = np.asarray(x_sample, np.float32)
    xp = np.asarray(x_prompt, np.float32)
    st = np.asarray(state_rec, np.float32)
    in_maps = []
    for core in range(8):
        s, half = core // 2, core % 2
        t0 = OWN * half - H
        seg = np.zeros((NS, D), np.float32)
        lo, hi = max(t0, 0), min(t0 + NS, 4096)
        seg[lo - t0:hi - t0] = xs[s, lo:hi]
        xpc = xp[2 * core:2 * core + 2].reshape(512, D)
        cc = np.stack([np.asarray(c[s], np.float32), np.asarray(c_ctx, np.float32)], -1)
        cf = np.ascontiguousarray(cc.reshape(8, 128, 2).transpose(1, 0, 2).reshape(128, 16))
        meta = np.zeros((128, 4), np.float32)
        meta[:, 0] = 32 * half - 3
        meta[:, 1] = 1.0 if half == 1 else 0.0
        meta[:, 2] = 1.0 if half == 0 else 0.0
        s0 = np.zeros((2, 8, 128, 128), np.float32)
        s0[half] = st[s, 0, half]
        m = dict(shared)
        m.update({"xsT": np.ascontiguousarray(seg.T), "xpT": np.ascontiguousarray(xpc.T), "cfm": cf, "meta": meta, "s0": s0})
        in_maps.append(m)
    nc = build()
    res = run_bass_kernel_spmd(nc, in_maps[:NCORES], core_ids=list(range(NCORES)))
    y_prompt = np.zeros((16, 256, D), np.float32)
    y_sample = np.zeros((4, 4096, D), np.float32)
    new_state = np.zeros((16, 1, 2, 8, 128, 128), np.float32)
    for core in range(NCORES):
        r = res.results[core]
        s, half = core // 2, core % 2
        y_sample[s, OWN * half:OWN * (half + 1)] = np.asarray(r["ysT"]).T
        y_prompt[2 * core:2 * core + 2] = np.asarray(r["ypT"]).T.reshape(2, 256, D)
        new_state[2 * core:2 * core + 2, 0] = np.asarray(r["nst"])
    return y_prompt, y_sample, new_state
```
